# Optimizing a Trainium2 kernel written in Bass

```python
import math
import jax, jax.numpy as jnp
from jax import lax
import numpy as np

D_MODEL = 2048
BATCH = 2
SEQ = 8192
DEPTH = 1

N_META = 16
BLOCK_Q = 128
LEAD_PAD = BLOCK_Q - N_META
RMS_EPS = 1e-6
NEG_INF = -1e30

DA_HEADS = 8
DA_HEAD_DIM = 64
DA_V_DIM = 2 * DA_HEAD_DIM
DA_LAMBDA_DIM = DA_HEAD_DIM

MLA_HEADS = 8
MLA_Q_RANK = 768
MLA_KV_RANK = 512
MLA_NOPE_DIM = 128
MLA_ROPE_DIM = 64
MLA_V_DIM = 128
ROPE_THETA = 10000.0

REL_BUCKETS = 32
REL_MAX_DIST = 128

D_FF = 5632

N_BRANCH = 2
DA_QK_W = DA_HEADS * 2 * DA_HEAD_DIM
DA_V_W = DA_HEADS * DA_V_DIM
MLA_OUT_W = MLA_HEADS * MLA_V_DIM
COL_SPLITS = (DA_QK_W, DA_QK_W, DA_V_W, MLA_Q_RANK, MLA_KV_RANK, MLA_ROPE_DIM, N_BRANCH * D_MODEL)
IN_PROJ_W = sum(COL_SPLITS)

kernel_name = "hybrid_diffattn_mla_gated_macaron"


def rmsnorm(x, g):
    xf = x.astype(jnp.float32)
    y = xf * lax.rsqrt(jnp.mean(xf * xf, axis=-1, keepdims=True) + RMS_EPS)
    return (y * g.astype(jnp.float32)).astype(x.dtype)


def swiglu(x, w_gate, w_up, w_down):
    return (jax.nn.silu(x @ w_gate) * (x @ w_up)) @ w_down


def rope(x, pos):
    half = x.shape[-1] // 2
    inv = ROPE_THETA ** (-jnp.arange(half, dtype=jnp.float32) * 2.0 / x.shape[-1])
    ang = pos.astype(jnp.float32)[:, None] * inv[None, :]
    ang = ang.reshape((1, pos.shape[0]) + (1,) * (x.ndim - 3) + (half,))
    cos, sin = jnp.cos(ang), jnp.sin(ang)
    xf = x.astype(jnp.float32)
    x1, x2 = xf[..., :half], xf[..., half:]
    return jnp.concatenate([x1 * cos - x2 * sin, x1 * sin + x2 * cos], axis=-1).astype(x.dtype)


def t5_causal_bucket(rel):
    n = jnp.maximum(rel, 0)
    max_exact = REL_BUCKETS // 2
    n_f = jnp.maximum(n, 1).astype(jnp.float32)
    large = max_exact + (jnp.log(n_f / max_exact) / math.log(REL_MAX_DIST / max_exact)
                         * (REL_BUCKETS - max_exact)).astype(jnp.int32)
    large = jnp.minimum(large, REL_BUCKETS - 1)
    return jnp.where(n < max_exact, n, large)


def causal_mask(q_pos, k_pos):
    kp = k_pos[None, :]
    qp = q_pos[:, None]
    return (kp <= qp) & ((kp >= LEAD_PAD) | (kp == qp))


def diff_attention(q, k, v, lam, lambda_init, sub_g, rel_bias_table):
    B, Lp, _ = q.shape
    n_blk = Lp // BLOCK_Q
    q = q.reshape(B, n_blk, BLOCK_Q, DA_HEADS, 2, DA_HEAD_DIM).transpose(1, 0, 3, 4, 2, 5)
    k = k.reshape(B, Lp, DA_HEADS, 2, DA_HEAD_DIM).transpose(0, 2, 3, 1, 4)
    v = v.reshape(B, Lp, DA_HEADS, DA_V_DIM).transpose(0, 2, 1, 3)
    k_pos = jnp.arange(Lp)
    scale = DA_HEAD_DIM ** -0.5
    table = rel_bias_table.astype(jnp.float32)

    def block(args):
        q_blk, b = args
        q_pos = b * BLOCK_Q + jnp.arange(BLOCK_Q)
        s = jnp.einsum("bhcqd,bhckd->bhcqk", q_blk, k, preferred_element_type=jnp.float32) * scale
        bias = table[t5_causal_bucket(q_pos[:, None] - k_pos[None, :])]
        s = s + jnp.transpose(bias, (2, 0, 1))[None, :, None]
        s = jnp.where(causal_mask(q_pos, k_pos), s, NEG_INF)
        p = jax.nn.softmax(s, axis=-1)
        a = p[:, :, 0] - lam * p[:, :, 1]
        return jnp.einsum("bhqk,bhkv->bhqv", a.astype(v.dtype), v)

    o = lax.map(block, (q, jnp.arange(n_blk)))
    o = o.transpose(1, 0, 3, 2, 4).reshape(B, Lp, DA_HEADS, DA_V_DIM)
    o = rmsnorm(o, sub_g) * (1.0 - lambda_init)
    return o.reshape(B, Lp, DA_V_W)


def mla_attention(c_q, c_kv, k_rope_in, q_norm_g, kv_norm_g, w_uq, w_ukv, pos):
    B, Lp, _ = c_q.shape
    n_blk = Lp // BLOCK_Q
    q = (rmsnorm(c_q, q_norm_g) @ w_uq).reshape(B, Lp, MLA_HEADS, MLA_NOPE_DIM + MLA_ROPE_DIM)
    q_nope = q[..., :MLA_NOPE_DIM]
    q_rope = rope(q[..., MLA_NOPE_DIM:], pos)
    kv = (rmsnorm(c_kv, kv_norm_g) @ w_ukv).reshape(B, Lp, MLA_HEADS, MLA_NOPE_DIM + MLA_V_DIM)
    k_nope, v = kv[..., :MLA_NOPE_DIM], kv[..., MLA_NOPE_DIM:]
    k_rope = rope(k_rope_in, pos)
    k_pos = jnp.arange(Lp)
    scale = (MLA_NOPE_DIM + MLA_ROPE_DIM) ** -0.5

    def to_blocks(t):
        return t.reshape((B, n_blk, BLOCK_Q) + t.shape[2:]).swapaxes(0, 1)

    def block(args):
        qn, qr, b = args
        q_pos = b * BLOCK_Q + jnp.arange(BLOCK_Q)
        s = (jnp.einsum("bqhd,bkhd->bhqk", qn, k_nope, preferred_element_type=jnp.float32)
             + jnp.einsum("bqhr,bkr->bhqk", qr, k_rope, preferred_element_type=jnp.float32)) * scale
        s = jnp.where(causal_mask(q_pos, k_pos), s, NEG_INF)
        p = jax.nn.softmax(s, axis=-1)
        return jnp.einsum("bhqk,bkhv->bqhv", p.astype(v.dtype), v)

    o = lax.map(block, (to_blocks(q_nope), to_blocks(q_rope), jnp.arange(n_blk)))
    return o.swapaxes(0, 1).reshape(B, Lp, MLA_OUT_W)


def setup_inputs(seed: int = 0) -> dict:
    key = jax.random.key(seed)
    ks = iter(jax.random.split(key, 40))
    f32 = jnp.float32

    def w(shape, fan_in):
        return jax.random.normal(next(ks), shape, f32) * (fan_in ** -0.5)

    def gain(shape):
        return 1.0 + 0.1 * jax.random.normal(next(ks), shape, f32)

    L = DEPTH
    return {
        "x": jax.random.normal(next(ks), (BATCH, SEQ, D_MODEL), f32),
        "meta_tokens": jax.random.normal(next(ks), (N_META, D_MODEL), f32),
        "rel_bias_table": 0.3 * jax.random.normal(next(ks), (REL_BUCKETS, DA_HEADS), f32),
        "ffn1_pre_g": gain((L, D_MODEL)),
        "ffn1_post_g": gain((L, D_MODEL)),
        "ffn1_w_gate": w((L, D_MODEL, D_FF), D_MODEL),
        "ffn1_w_up": w((L, D_MODEL, D_FF), D_MODEL),
        "ffn1_w_down": w((L, D_FF, D_MODEL), D_FF),
        "mix_pre_g": gain((L, D_MODEL)),
        "mix_post_g": gain((L, D_MODEL)),
        "w_in": w((L, D_MODEL, IN_PROJ_W), D_MODEL),
        "b_gate": 0.1 * jax.random.normal(next(ks), (L, N_BRANCH * D_MODEL), f32),
        "da_lambda_q1": 0.1 * jax.random.normal(next(ks), (L, DA_LAMBDA_DIM), f32),
        "da_lambda_k1": 0.1 * jax.random.normal(next(ks), (L, DA_LAMBDA_DIM), f32),
        "da_lambda_q2": 0.1 * jax.random.normal(next(ks), (L, DA_LAMBDA_DIM), f32),
        "da_lambda_k2": 0.1 * jax.random.normal(next(ks), (L, DA_LAMBDA_DIM), f32),
        "da_sub_g": gain((L, DA_V_DIM)),
        "mla_q_norm_g": gain((L, MLA_Q_RANK)),
        "mla_kv_norm_g": gain((L, MLA_KV_RANK)),
        "mla_w_uq": w((L, MLA_Q_RANK, MLA_HEADS * (MLA_NOPE_DIM + MLA_ROPE_DIM)), MLA_Q_RANK),
        "mla_w_ukv": w((L, MLA_KV_RANK, MLA_HEADS * (MLA_NOPE_DIM + MLA_V_DIM)), MLA_KV_RANK),
        "w_branch_da": w((L, DA_V_W, D_MODEL), DA_V_W),
        "w_branch_mla": w((L, MLA_OUT_W, D_MODEL), MLA_OUT_W),
        "w_out": w((L, D_MODEL, D_MODEL), D_MODEL),
        "ffn2_pre_g": gain((L, D_MODEL)),
        "ffn2_post_g": gain((L, D_MODEL)),
        "ffn2_w_gate": w((L, D_MODEL, D_FF), D_MODEL),
        "ffn2_w_up": w((L, D_MODEL, D_FF), D_MODEL),
        "ffn2_w_down": w((L, D_FF, D_MODEL), D_FF),
    }


def reference(x, meta_tokens, rel_bias_table,
              ffn1_pre_g, ffn1_post_g, ffn1_w_gate, ffn1_w_up, ffn1_w_down,
              mix_pre_g, mix_post_g, w_in, b_gate,
              da_lambda_q1, da_lambda_k1, da_lambda_q2, da_lambda_k2, da_sub_g,
              mla_q_norm_g, mla_kv_norm_g, mla_w_uq, mla_w_ukv,
              w_branch_da, w_branch_mla, w_out,
              ffn2_pre_g, ffn2_post_g, ffn2_w_gate, ffn2_w_up, ffn2_w_down):
    B = x.shape[0]
    pad = jnp.zeros((B, LEAD_PAD, D_MODEL), x.dtype)
    meta = jnp.broadcast_to(meta_tokens.astype(x.dtype)[None], (B, N_META, D_MODEL))
    h = jnp.concatenate([pad, meta, x], axis=1)
    Lp = h.shape[1]
    pos = jnp.arange(Lp) - LEAD_PAD
    offs = np.cumsum(np.array(COL_SPLITS))[:-1].tolist()

    for l in range(DEPTH):
        f = swiglu(rmsnorm(h, ffn1_pre_g[l]), ffn1_w_gate[l], ffn1_w_up[l], ffn1_w_down[l])
        h = h + 0.5 * rmsnorm(f, ffn1_post_g[l])

        hn = rmsnorm(h, mix_pre_g[l])
        proj = hn @ w_in[l]
        da_q, da_k, da_v, mla_cq, mla_ckv, mla_kr, gate_logits = jnp.split(proj, offs, axis=-1)

        lambda_init = 0.8 - 0.6 * math.exp(-0.3 * l)
        lam = (jnp.exp(jnp.sum(da_lambda_q1[l].astype(jnp.float32) * da_lambda_k1[l].astype(jnp.float32)))
               - jnp.exp(jnp.sum(da_lambda_q2[l].astype(jnp.float32) * da_lambda_k2[l].astype(jnp.float32)))
               + lambda_init)
        y_da = diff_attention(da_q, da_k, da_v, lam, lambda_init, da_sub_g[l], rel_bias_table)
        y_mla = mla_attention(mla_cq, mla_ckv, mla_kr, mla_q_norm_g[l], mla_kv_norm_g[l],
                              mla_w_uq[l], mla_w_ukv[l], pos)

        gates = jax.nn.sigmoid((gate_logits + b_gate[l]).astype(jnp.float32)).reshape(B, Lp, N_BRANCH, D_MODEL)
        merged = (gates[:, :, 0] * (y_da @ w_branch_da[l]).astype(jnp.float32)
                  + gates[:, :, 1] * (y_mla @ w_branch_mla[l]).astype(jnp.float32)).astype(h.dtype)
        m = merged @ w_out[l]
        h = h + rmsnorm(m, mix_post_g[l])

        f = swiglu(rmsnorm(h, ffn2_pre_g[l]), ffn2_w_gate[l], ffn2_w_up[l], ffn2_w_down[l])
        h = h + 0.5 * rmsnorm(f, ffn2_post_g[l])

    return h[:, LEAD_PAD + N_META:]
```

```python
import math
from contextlib import ExitStack
import numpy as np
import concourse.bass as bass
import concourse.mybir as mybir
from concourse.bass_utils import run_bass_kernel_spmd

F32 = mybir.dt.float32
BF16 = mybir.dt.bfloat16
AF = mybir.ActivationFunctionType
ALU = mybir.AluOpType

D = 2048
KT = D // 128
NH = 8
EPS = 1e-6
NEG = -30000.0
NSLOT = 9
INW = 8512
C_DQ, C_DK, C_DV, C_CQ, C_CKV, C_KR, C_G = 0, 1024, 2048, 3072, 3840, 4352, 4416


class Buf:
    __slots__ = ("w", "r", "name", "excl")

    def __init__(self, name="", excl=False):
        self.w = None
        self.r = {}
        self.name = name
        self.excl = excl


class Q:
    def __init__(self, name, sem, selfwait):
        self.name = name
        self.sem = sem
        self.cnt = 0
        self.seen = {}
        self.selfwait = selfwait
        self.prog = []
        self.trace = []


class DSem:
    def __init__(self, sem):
        self.sem = sem
        self.cnt = 0


class Sched:
    def __init__(self, nc, es):
        self.nc = nc
        self.es = es
        self.q = {}
        for name, sw in (("pe", False), ("act", True), ("dve", True), ("pool", True), ("sp", False)):
            self.q[name] = Q(name, es.enter_context(nc.semaphore("q_" + name)), sw)
        self.dsems = []
        self.tag = ""

    def simulate(self):
        sems = {}
        pc = {n: 0 for n in self.q}
        prog = True
        while prog:
            prog = False
            for n, q in self.q.items():
                while pc[n] < len(q.trace):
                    kind, sid, val, nm = q.trace[pc[n]]
                    if kind == "w":
                        if sems.get(sid, 0) < val:
                            break
                    elif kind == "i":
                        sems[sid] = sems.get(sid, 0) + val
                    pc[n] += 1
                    prog = True
        stuck = {n: (pc[n], len(q.trace)) for n, q in self.q.items() if pc[n] < len(q.trace)}
        for n in stuck:
            q = self.q[n]
            kind, sid, val, nm = q.trace[pc[n]]
            prev = [t[3] for t in q.trace[max(0, pc[n] - 3):pc[n]] if t[0] != "w"]
            nxt = [t[3] for t in q.trace[pc[n]:pc[n] + 6] if t[0] != "w"]
            print("STUCK", n, pc[n], "/", len(q.trace), "waits", nm, "val", val, "have", sems.get(sid, 0), "prev", prev, "next", nxt)
        return not stuck

    def dsem(self, name):
        d = DSem(self.es.enter_context(self.nc.semaphore("d_" + name + str(len(self.dsems)))))
        self.dsems.append(d)
        return d

    def _wait(self, q, ev):
        kind, key, val = ev
        if kind == "eng" and key is q and not q.selfwait:
            return
        k = id(key)
        if q.seen.get(k, 0) >= val:
            return
        q.seen[k] = val
        sem = key.sem
        q.prog.append(lambda e, sem=sem, val=val: e.wait_ge(sem, val))
        q.trace.append(("w", id(key), val, getattr(key, "name", "dsem")))

    def _sync(self, q, reads, writes):
        for b in reads:
            if b.w is not None:
                self._wait(q, b.w)
        for b in writes:
            if b.w is not None:
                self._wait(q, b.w)
            for r in b.r.values():
                self._wait(q, r)

    @staticmethod
    def _commit(ev, reads, writes):
        for b in writes:
            b.w = ev
            b.r = {}
        for b in reads:
            k = id(ev[1])
            o = b.r.get(k)
            if o is None or o[2] < ev[2]:
                b.r[k] = ev

    def op(self, qn, fn, reads=(), writes=(), inc=True):
        q = self.q[qn]
        ex = [b for b in reads if b.excl]
        if ex:
            reads = [b for b in reads if not b.excl]
            writes = list(writes) + ex
        self._sync(q, reads, writes)
        if inc:
            q.cnt += 1
            sem = q.sem
            q.prog.append(lambda e, fn=fn, sem=sem: fn(e).then_inc(sem, 1))
            q.trace.append(("i", id(q), 1, self.tag))
            ev = ("eng", q, q.cnt)
        else:
            q.prog.append(lambda e, fn=fn: fn(e))
            q.trace.append(("n", 0, 0, self.tag))
            ev = ("eng", q, q.cnt + 1)
        self._commit(ev, reads, writes)

    def dma(self, qn, fns, reads, writes, ds):
        q = self.q[qn]
        self._sync(q, reads, writes)
        for fn in fns:
            ds.cnt += 16
            sem = ds.sem
            q.prog.append(lambda e, fn=fn, sem=sem: fn(e).then_inc(sem, 16))
            q.trace.append(("i", id(ds), 16, self.tag))
        self._commit(("dma", ds, ds.cnt), reads, writes)

    def barrier(self):
        sp = self.q["sp"]
        for d in self.dsems:
            if d.cnt > 0:
                self._wait(sp, ("dma", d, d.cnt))
        for n in ("pe", "act", "dve", "pool"):
            if self.q[n].cnt > 0:
                self._wait(sp, ("eng", self.q[n], self.q[n].cnt))
        sp.cnt += 1
        sem = sp.sem
        sp.prog.append(lambda e, sem=sem: e.nop().then_inc(sem, 1))
        sp.trace.append(("i", id(sp), 1, "barrier"))
        for n in ("pe", "act", "dve", "pool"):
            self._wait(self.q[n], ("eng", sp, sp.cnt))


def build(SEQ, DFF, STOP=99):
    NT = SEQ // 512
    NOWN = SEQ // 4
    NOT_ = NOWN // 512
    NK = 16 + SEQ
    NB = 1 + SEQ // 128
    FT = DFF // 128
    FH = FT // 2

    nc = bass.Bass("TRN2", target_bir_lowering=False)
    es = ExitStack()
    S = Sched(nc, es)
    _cnt = [0]

    def sbt(name, shape, dt):
        _cnt[0] += 1
        return nc.sbuf_tensor("%s_%d" % (name, _cnt[0]), shape, dt)

    def din(name, shape, dt=F32):
        return nc.dram_tensor(name, list(shape), dt, kind="ExternalInput").ap()

    def dscr(name, shape, dt=BF16):
        import os
        if "DBG_OUT" in os.environ and not name.startswith("wb_") and not name.startswith("wt_") and name != "toep":
            return nc.dram_tensor(name, list(shape), dt, kind="ExternalOutput").ap()
        return nc.dram_tensor(name, list(shape), dt).ap()

    xT = din("xT", [D, SEQ])
    metaT = din("metaT", [D, 16])
    w_f = {}
    wshape = {"f1g": (D, DFF), "f1u": (D, DFF), "f1d": (DFF, D), "f2g": (D, DFF), "f2u": (D, DFF), "f2d": (DFF, D),
              "win": (D, INW), "wuq": (768, 1536), "wukv": (512, 2048), "wbd": (1024, D), "wbm": (1024, D), "wo": (D, D)}
    import os
    NOW = "DBG_NOW" in os.environ
    for n, shp in wshape.items():
        if not NOW:
            w_f[n] = din("w_" + n, shp)
    if NOW:
        wsmall = din("wsmall", [2048, 2048])
    gains = din("gains", [128, 6 * KT + 32 + 6 + 4 + 1])
    lamv = din("lamv", [128, 4 * 64])
    tab = din("tab", [32, 8])
    oh = din("oh", [33, NSLOT * 256])
    ident_in = din("ident", [128, 128])
    rot_in = din("rot", [64, 64])
    cosk = din("cosk", [64, NK])
    sink = din("sink", [64, NK])
    cosq = din("cosq", [64, NOWN])
    sinq = din("sinq", [64, NOWN])
    outT = nc.dram_tensor("outT", [D, NOWN], F32, kind="ExternalOutput").ap()

    w_b = {n: dscr("wb_" + n, shp) for n, shp in wshape.items()}
    tile_specs = {}

    def _add(w, r0, nk, c0, ncols):
        tile_specs.setdefault(w, []).append((r0, nk, c0, ncols))
    _nhalf = 2 if FT > 32 else 1
    _kh = FT // _nhalf
    for pre_ in ("f1", "f2"):
        for f_ in range(FT):
            _add(pre_ + "g", 0, KT, f_ * 128, 128)
            _add(pre_ + "u", 0, KT, f_ * 128, 128)
        for o_ in range(KT):
            for hh_ in range(_nhalf):
                _add(pre_ + "d", hh_ * _kh * 128, _kh, o_ * 128, 128)
    for base_, n_ in ((C_DQ, 4), (C_DK, 4), (C_DV, 4), (C_CQ, 3), (C_CKV, 2), (C_G, 16)):
        for i_ in range(n_):
            _add("win", 0, KT, base_ + i_ * 256, 256)
    _add("win", 0, KT, C_KR, 64)
    for o_ in range(KT):
        _add("wbd", 0, 8, o_ * 128, 128)
        _add("wbm", 0, 8, o_ * 128, 128)
        _add("wo", 0, KT, o_ * 128, 128)
    w_t = {}
    tidx = {}
    for w_, lst_ in tile_specs.items():
        w_t[w_] = dscr("wt_" + w_, [len(lst_), 128, max(nk_ * nc_ for (_, nk_, _, nc_) in lst_)])
        for i_, (r0_, nk_, c0_, nc_) in enumerate(lst_):
            tidx[(w_, r0_, c0_)] = i_
    kT_da = dscr("kT_da", [NH, 128, NK])
    v_da = dscr("v_da", [NB, 128, 1024])
    kT_no = dscr("kT_no", [NH, 128, NK])
    kT_ro = dscr("kT_ro", [64, NK])
    v_ml = dscr("v_ml", [NB, 128, 1024])
    qT_da = dscr("qT_da", [NH, 128, NOWN])
    qT_no = dscr("qT_no", [NH, 128, NOWN])
    qT_ro = dscr("qT_ro", [NH, 64, NOWN])
    gat = dscr("gat", [32, 128, NOWN])
    h1o = dscr("h1o", [KT, 128, NOWN], F32)
    y_da = dscr("y_da", [NH, 128, NOWN])
    y_ml = dscr("y_ml", [NH, 128, NOWN])
    toep = dscr("toep", [9 * NSLOT, 128, 256], F32)

    def sb(name, shape, dt):
        return es.enter_context(sbt(name, list(shape), dt))

    gsb = sb("gsb", [128, 6 * KT + 43], F32)
    ones_b = sb("ones_b", [128, 128], BF16)
    ident_b = sb("ident_b", [128, 128], BF16)
    rot_f = sb("rot_f", [64, 64], F32)
    neglam = sb("neglam", [128, 1], F32)
    subg8 = sb("subg8", [128, 1], F32)
    epsb = sb("epsb", [128, 1], F32)
    B_const = Buf("const")
    psum = [es.enter_context(nc.psum_tensor("ps%d" % i, [128, 512], F32)) for i in range(8)]
    PB = [Buf("ps%d" % i, excl=True) for i in range(8)]
    G_F1PRE, G_F1POST, G_MIXPRE, G_MIXPOST, G_F2PRE, G_F2POST = [i * KT for i in range(6)]
    G_BG = 6 * KT
    G_QN = G_BG + 32
    G_KVN = G_QN + 6
    G_SUBG = G_KVN + 4

    d_const = S.dsem("const")

    S.dma("sp", [lambda e: e.dma_start(out=gsb[:], in_=gains),
                 lambda e: e.dma_start(out=rot_f[:], in_=rot_in)], [], [B_const], d_const)
    S.op("dve", lambda e: e.memset(ones_b[:], 1.0), [], [B_const])
    S.op("dve", lambda e: e.memset(epsb[:], EPS), [], [B_const])
    S.dma("pool", [lambda e: e.dma_start(out=ident_b[:], in_=ident_in)], [], [B_const], S.dsem("ident"))
    d_conv = S.dsem("conv")
    conv_fns = []
    for n, shp in wshape.items():
        rows = shp[0]
        step = 256
        for r0 in range(0, rows, step):
            r1 = min(rows, r0 + step)
            conv_fns.append(lambda e, n=n, r0=r0, r1=r1: e.dma_start(out=w_b[n][r0:r1, :], in_=w_f[n][r0:r1, :]))
    if NOW:
        conv_fns = []
        for n, shp in wshape.items():
            for r0 in range(0, shp[0], 256):
                r1 = min(shp[0], r0 + 256)
                for c0 in range(0, shp[1], 2048):
                    c1 = min(shp[1], c0 + 2048)
                    conv_fns.append(lambda e, n=n, r0=r0, r1=r1, c0=c0, c1=c1: e.dma_start(
                        out=w_b[n][r0:r1, c0:c1], in_=wsmall[r0 % 2048:r0 % 2048 + (r1 - r0), 0:c1 - c0]))
    B_wb = Buf("wb")
    B_wt = Buf("wt")
    S.dma("pool", conv_fns, [], [B_wb], d_conv)
    d_tile = S.dsem("tile")
    tile_fns = []
    for w_, lst_ in tile_specs.items():
        for i_, (r0_, nk_, c0_, nc_) in enumerate(lst_):
            tile_fns.append(lambda e, w_=w_, i_=i_, r0_=r0_, nk_=nk_, c0_=c0_, nc_=nc_: e.dma_start(
                out=w_t[w_][i_, :, 0:nk_ * nc_].rearrange("p (k c) -> p k c", k=nk_),
                in_=w_b[w_][r0_:r0_ + nk_ * 128, c0_:c0_ + nc_].rearrange("(k p) c -> p k c", p=128)))
    S.dma("sp", tile_fns, [B_wb], [B_wt], d_tile)

    with ExitStack() as ph:
        lv = ph.enter_context(sbt("lv", [128, 256], F32))
        lt = ph.enter_context(sbt("lt", [128, 128], F32))
        ls = ph.enter_context(sbt("ls", [128, 4], F32))
        B_l = Buf("lam")
        S.dma("sp", [lambda e: e.dma_start(out=lv[:], in_=lamv)], [], [B_l], S.dsem("lamv"))
        S.op("dve", lambda e: e.tensor_tensor(out=lt[:, 0:64], in0=lv[:, 0:64], in1=lv[:, 64:128], op=ALU.mult), [B_l], [B_l])
        S.op("dve", lambda e: e.tensor_tensor(out=lt[:, 64:128], in0=lv[:, 128:192], in1=lv[:, 192:256], op=ALU.mult), [B_l], [B_l])
        S.op("dve", lambda e: e.reduce_sum(out=ls[:, 0:1], in_=lt[:, 0:64], axis=mybir.AxisListType.X), [B_l], [B_l])
        S.op("dve", lambda e: e.reduce_sum(out=ls[:, 1:2], in_=lt[:, 64:128], axis=mybir.AxisListType.X), [B_l], [B_l])
        S.op("act", lambda e: e.activation(out=ls[:, 2:4], in_=ls[:, 0:2], func=AF.Exp), [B_l], [B_l])
        S.op("dve", lambda e: e.scalar_tensor_tensor(out=neglam[:], in0=ls[:, 3:4], scalar=-0.2, in1=ls[:, 2:3],
                                                     op0=ALU.add, op1=ALU.subtract), [B_l], [B_l, B_const])
        S.op("dve", lambda e: e.tensor_scalar(out=subg8[:], in0=gsb[:, G_SUBG:G_SUBG + 1], scalar1=0.8, scalar2=None,
                                              op0=ALU.mult), [B_const], [B_const])
        S.barrier()

    NRING = 4

    class Ctx:
        pass

    def alloc_main(ph, need_ffn=True, need_stg=True):
        c = Ctx()
        c.ring = ph.enter_context(sbt("ring", [128, NRING, 4096], BF16))
        c.ringB = [Buf("ring%d" % i) for i in range(NRING)]
        c.ringD = [S.dsem("ring") for _ in range(NRING)]
        c.rp = 0
        c.xt = ph.enter_context(sbt("xt", [128, KT, 512], F32))
        c.xtB = [Buf("xt%d" % k) for k in range(KT)]
        c.xtD = S.dsem("xt")
        c.xn = ph.enter_context(sbt("xn", [128, KT, 512], BF16))
        c.xnB = [Buf("xn%d" % k) for k in range(KT)]
        if need_ffn:
            c.act = ph.enter_context(sbt("act", [128, max(FT, 4), 512], BF16))
            c.actB = [Buf("act%d" % k) for k in range(max(FT, 4))]
            c.ft = ph.enter_context(sbt("ft", [128, KT, 512], F32))
            c.ftB = [Buf("ft%d" % k) for k in range(KT)]
        c.sq = ph.enter_context(sbt("sq", [128, 4, 512], BF16))
        c.sqB = [Buf("sq%d" % k) for k in range(4)]
        c.sqp = 0
        c.rstd = ph.enter_context(sbt("rstd", [128, 512], F32))
        c.rstdB = Buf("rstd")
        c.tmp = ph.enter_context(sbt("tmp", [128, 2, 512], F32))
        c.tmpB = [Buf("tmp0"), Buf("tmp1")]
        c.tp = 0
        if need_stg:
            c.stg = ph.enter_context(sbt("stg", [128, 4, 512], BF16))
            c.stgB = [Buf("stg%d" % k) for k in range(4)]
            c.stgD = [S.dsem("stg") for _ in range(4)]
        c.sp_ = 0
        return c

    def ring_load(c, fns_for_slot):
        s = c.rp % NRING
        c.rp += 1
        S.dma("sp", fns_for_slot(s), [], [c.ringB[s]], c.ringD[s])
        return s

    def wload(c, wname, r0, nk, c0, ncols):
        w = w_b[wname]
        if (wname, r0, c0) in tidx:
            ti = tidx[(wname, r0, c0)]

            def fns_t(s):
                return [lambda e: e.dma_start(out=c.ring[:, s, 0:nk * ncols], in_=w_t[wname][ti, :, 0:nk * ncols])]
            return ring_load(c, fns_t)

        def fns(s):
            return [lambda e: e.dma_start(
                out=c.ring[:, s, 0:nk * ncols].rearrange("p (k c) -> p k c", k=nk),
                in_=w[r0:r0 + nk * 128, c0:c0 + ncols].rearrange("(k p) c -> p k c", p=128))]
        return ring_load(c, fns)

    def norm_stats(c, src, srcB, nk, W, pb, dim):
        for k in range(nk):
            i = c.sqp % 4
            c.sqp += 1
            S.op("act", lambda e, k=k, i=i: e.activation(out=c.sq[:, i, 0:W], in_=src[:, k, 0:W], func=AF.Square),
                 [srcB[k]], [c.sqB[i]])
            S.op("pe", lambda e, k=k, i=i: e.matmul(psum[pb][:, 0:W], ones_b[:], c.sq[:, i, 0:W], start=(k == 0), stop=(k == nk - 1)),
                 [c.sqB[i], B_const], [PB[pb]], inc=True)
        S.op("act", lambda e: e.activation(out=c.rstd[:, 0:W], in_=psum[pb][:, 0:W], func=AF.Sqrt, scale=1.0 / dim, bias=epsb[:]),
             [PB[pb], B_const], [c.rstdB])
        S.op("dve", lambda e: e.reciprocal(out=c.rstd[:, 0:W], in_=c.rstd[:, 0:W]), [c.rstdB], [c.rstdB])

    def normalize(c, src, srcB, dst, dstB, nk, W, gcol):
        for k in range(nk):
            S.op("dve", lambda e, k=k: e.scalar_tensor_tensor(out=dst[:, k, 0:W], in0=src[:, k, 0:W], scalar=gsb[:, gcol + k:gcol + k + 1],
                                                             in1=c.rstd[:, 0:W], op0=ALU.mult, op1=ALU.mult),
                 [srcB[k], c.rstdB, B_const], [dstB[k]])

    def residual(c, W, gcol, half):
        for k in range(KT):
            t = c.tp % 2
            c.tp += 1
            S.op("pool", lambda e, k=k, t=t: e.tensor_tensor(out=c.tmp[:, t, 0:W], in0=c.ft[:, k, 0:W], in1=c.rstd[:, 0:W], op=ALU.mult),
                 [c.ftB[k], c.rstdB], [c.tmpB[t]])
            if half:
                S.op("dve", lambda e, k=k, t=t: e.tensor_scalar(out=c.tmp[:, t, 0:W], in0=c.tmp[:, t, 0:W], scalar1=gsb[:, gcol + k:gcol + k + 1],
                                                                scalar2=0.5, op0=ALU.mult, op1=ALU.mult), [c.tmpB[t], B_const], [c.tmpB[t]])
            else:
                S.op("dve", lambda e, k=k, t=t: e.tensor_scalar(out=c.tmp[:, t, 0:W], in0=c.tmp[:, t, 0:W], scalar1=gsb[:, gcol + k:gcol + k + 1],
                                                                scalar2=None, op0=ALU.mult), [c.tmpB[t], B_const], [c.tmpB[t]])
            S.op("dve", lambda e, k=k, t=t: e.tensor_tensor(out=c.xt[:, k, 0:W], in0=c.xt[:, k, 0:W], in1=c.tmp[:, t, 0:W], op=ALU.add),
                 [c.tmpB[t], c.xtB[k]], [c.xtB[k]])

    def down_like(c, wname, nkt, rhs, rhsB, W, pbs, ss_pb):
        nhalf = 2 if nkt > 32 else 1
        kh = nkt // nhalf
        for o in range(KT):
            slots = [wload(c, wname, hh * kh * 128, kh, o * 128, 128) for hh in range(nhalf)]
            pb = pbs[o % len(pbs)]
            for k in range(nkt):
                s = slots[k // kh]
                kk = k % kh
                S.op("pe", lambda e, s=s, kk=kk, k=k, pb=pb: e.matmul(psum[pb][:, 0:W], c.ring[:, s, kk * 128:(kk + 1) * 128], rhs[:, k, 0:W],
                                                                     start=(k == 0), stop=(k == nkt - 1)),
                     [c.ringB[s], rhsB[k]], [PB[pb]], inc=(k == nkt - 1))
            S.op("dve", lambda e, o=o, pb=pb: e.tensor_copy(out=c.ft[:, o, 0:W], in_=psum[pb][:, 0:W]), [PB[pb]], [c.ftB[o]])
            i = c.sqp % 4
            c.sqp += 1
            S.op("act", lambda e, i=i, pb=pb: e.activation(out=c.sq[:, i, 0:W], in_=psum[pb][:, 0:W], func=AF.Square), [PB[pb]], [c.sqB[i]])
            import os
            DL = int(os.environ.get("DL", "0"))
            S.op("pe", lambda e, i=i, o=o: e.matmul(psum[ss_pb][:, 0:W], ones_b[:], c.sq[:, i, 0:W], start=(o == 0 or DL == 1), stop=(o == KT - 1 or DL == 1)),
                 [c.sqB[i], B_const], [PB[ss_pb]], inc=True)
        S.op("act", lambda e: e.activation(out=c.rstd[:, 0:W], in_=psum[ss_pb][:, 0:W], func=AF.Sqrt, scale=1.0 / D, bias=epsb[:]),
             [PB[ss_pb], B_const], [c.rstdB])
        S.op("dve", lambda e: e.reciprocal(out=c.rstd[:, 0:W], in_=c.rstd[:, 0:W]), [c.rstdB], [c.rstdB])

    def ffn(c, W, pre, gpre, gpost):
        norm_stats(c, c.xt, c.xtB, KT, W, 7, D)
        normalize(c, c.xt, c.xtB, c.xn, c.xnB, KT, W, gpre)
        import os
        F1S = int(os.environ.get("F1S", "9"))
        if F1S < 2:
            return
        for f in range(FT):
            def fns(s, f=f):
                return [lambda e: e.dma_start(out=c.ring[:, s, 0:2048], in_=w_t[pre + "g"][f, :, 0:2048]),
                        lambda e: e.dma_start(out=c.ring[:, s, 2048:4096], in_=w_t[pre + "u"][f, :, 0:2048])]
            s = ring_load(c, fns)
            pg = 2 * (f % 2)
            pu = pg + 1
            for k in range(KT):
                S.op("pe", lambda e, s=s, k=k, pg=pg: e.matmul(psum[pg][:, 0:W], c.ring[:, s, k * 128:(k + 1) * 128], c.xn[:, k, 0:W],
                                                               start=(k == 0), stop=(k == KT - 1)), [c.ringB[s], c.xnB[k]], [PB[pg]], inc=(k == KT - 1))
            for k in range(KT):
                S.op("pe", lambda e, s=s, k=k, pu=pu: e.matmul(psum[pu][:, 0:W], c.ring[:, s, 2048 + k * 128:2048 + (k + 1) * 128], c.xn[:, k, 0:W],
                                                               start=(k == 0), stop=(k == KT - 1)), [c.ringB[s], c.xnB[k]], [PB[pu]], inc=(k == KT - 1))
            t = c.tp % 2
            c.tp += 1
            S.op("act", lambda e, t=t, pg=pg: e.activation(out=c.tmp[:, t, 0:W], in_=psum[pg][:, 0:W], func=AF.Silu), [PB[pg]], [c.tmpB[t]])
            S.op("dve", lambda e, t=t, pu=pu, f=f: e.tensor_tensor(out=c.act[:, f, 0:W], in0=c.tmp[:, t, 0:W], in1=psum[pu][:, 0:W], op=ALU.mult),
                 [c.tmpB[t], PB[pu]], [c.actB[f]])
        if F1S < 3:
            return
        down_like(c, pre + "d", FT, c.act, c.actB, W, [4, 5], 6)
        if F1S < 4:
            return
        residual(c, W, gpost, True)

    def stage_store(c, rows, W, src_fn, srcBs, dst_ap_fn):
        i = c.sp_ % 4
        c.sp_ += 1
        src_fn(i)
        S.dma("pool", [lambda e: e.dma_start(out=dst_ap_fn(), in_=c.stg[0:rows, i, 0:W])], [c.stgB[i]], [], c.stgD[i])

    def proj_fm(c, W, col0, npairs, consume):
        for hp in range(npairs):
            s = wload(c, "win", 0, KT, col0 + hp * 256, 256)
            for hh in range(2):
                j = hp * 2 + hh
                pb = j % 4
                for k in range(KT):
                    S.op("pe", lambda e, s=s, k=k, hh=hh, pb=pb: e.matmul(psum[pb][:, 0:W], c.ring[:, s, k * 256 + hh * 128:k * 256 + hh * 128 + 128],
                                                                        c.xn[:, k, 0:W], start=(k == 0), stop=(k == KT - 1)),
                         [c.ringB[s], c.xnB[k]], [PB[pb]], inc=(k == KT - 1))
                consume(j, pb)

    def copy_store(c, rows, W, pb, dst_fn):
        def ev(i):
            S.op("act", lambda e: e.activation(out=c.stg[0:rows, i, 0:W], in_=psum[pb][0:rows, 0:W], func=AF.Copy), [PB[pb]], [c.stgB[i]])
        stage_store(c, rows, W, ev, None, dst_fn)

    def rope_store(c, W, src, srcB, pbr, cs, csB, dst_fn):
        S.op("pe", lambda e: e.matmul(psum[pbr][0:64, 0:W], rot_f[:], src, start=True, stop=True), [srcB, B_const], [PB[pbr]])
        t0_ = c.tp % 2
        c.tp += 1
        S.op("dve", lambda e: e.tensor_tensor(out=c.tmp[0:64, t0_, 0:W], in0=src, in1=cs[:, 0, 0:W], op=ALU.mult), [srcB, csB], [c.tmpB[t0_]])
        t1_ = c.tp % 2
        c.tp += 1
        S.op("dve", lambda e: e.tensor_tensor(out=c.tmp[0:64, t1_, 0:W], in0=psum[pbr][0:64, 0:W], in1=cs[:, 1, 0:W], op=ALU.mult),
             [PB[pbr], csB], [c.tmpB[t1_]])

        def ev(i):
            S.op("dve", lambda e: e.tensor_tensor(out=c.stg[0:64, i, 0:W], in0=c.tmp[0:64, t0_, 0:W], in1=c.tmp[0:64, t1_, 0:W], op=ALU.add),
                 [c.tmpB[t0_], c.tmpB[t1_]], [c.stgB[i]])
        stage_store(c, 64, W, ev, None, dst_fn)

    def tileA1(c, t, cs, csB, csD):
        W = 16 if t < 0 else 512
        kc0 = 0 if t < 0 else 16 + t * 512
        nblk = 1 if t < 0 else 4
        rows = 16 if t < 0 else 128
        src = metaT if t < 0 else xT[:, t * 512:(t + 1) * 512]
        S.dma("pool", [lambda e: e.dma_start(out=c.xt[:, :, 0:W], in_=src.rearrange("(k p) w -> p k w", p=128))], [], c.xtB, c.xtD)
        S.dma("pool", [lambda e: e.dma_start(out=cs[:, 0, 0:W], in_=cosk[:, kc0:kc0 + W]),
                       lambda e: e.dma_start(out=cs[:, 1, 0:W], in_=sink[:, kc0:kc0 + W])], [], [csB], csD)
        import os
        A1S = int(os.environ.get("A1S", "9"))
        if A1S >= 1:
            ffn(c, W, "f1", G_F1PRE, G_F1POST)
        if A1S < 2:
            return
        if t >= 0:
            S.dma("pool", [lambda e: e.dma_start(out=h1o[:, :, t * 128:(t + 1) * 128].rearrange("k p w -> p k w"), in_=c.xt[:, :, 0:128])],
                  c.xtB, [], c.xtD)
        norm_stats(c, c.xt, c.xtB, KT, W, 7, D)
        normalize(c, c.xt, c.xtB, c.xn, c.xnB, KT, W, G_MIXPRE)
        proj_fm(c, W, C_DK, 4, lambda h, pb: copy_store(c, 128, W, pb, lambda: kT_da[h, :, kc0:kc0 + W]))
        if A1S < 3:
            return
        for qd in range(4):
            s = wload(c, "win", 0, KT, C_DV + qd * 256, 256)
            for j in range(nblk):
                pb = j % 4
                for k in range(KT):
                    S.op("pe", lambda e, s=s, k=k, j=j, pb=pb: e.matmul(psum[pb][0:rows, 0:256], c.xn[:, k, j * 128:j * 128 + rows],
                                                                      c.ring[:, s, k * 256:(k + 1) * 256], start=(k == 0), stop=(k == KT - 1)),
                         [c.ringB[s], c.xnB[k]], [PB[pb]], inc=(k == KT - 1))
                blk = 0 if t < 0 else 1 + t * 4 + j
                copy_store(c, rows, 256, pb, lambda blk=blk, qd=qd: v_da[blk, 0:rows, qd * 256:(qd + 1) * 256])
        if A1S < 4:
            return
        proj_fm(c, W, C_CKV, 2, lambda j, pb: S.op("dve", lambda e: e.tensor_copy(out=c.ft[:, j, 0:W], in_=psum[pb][:, 0:W]), [PB[pb]], [c.ftB[j]]))
        s = wload(c, "win", 0, KT, C_KR, 64)
        for k in range(KT):
            S.op("pe", lambda e, k=k, s=s: e.matmul(psum[4][0:64, 0:W], c.ring[:, s, k * 64:(k + 1) * 64], c.xn[:, k, 0:W],
                                               start=(k == 0), stop=(k == KT - 1)), [c.ringB[s], c.xnB[k]], [PB[4]], inc=(k == KT - 1))
        S.op("dve", lambda e: e.tensor_copy(out=c.ft[0:64, 4, 0:W], in_=psum[4][0:64, 0:W]), [PB[4]], [c.ftB[4]])
        rope_store(c, W, c.ft[0:64, 4, 0:W], c.ftB[4], 5, cs, csB, lambda: kT_ro[:, kc0:kc0 + W])
        if A1S < 5:
            return
        norm_stats(c, c.ft, c.ftB, 4, W, 7, 512)
        normalize(c, c.ft, c.ftB, c.act, c.actB, 4, W, G_KVN)
        sl = [wload(c, "wukv", 0, 4, hf * 1024, 1024) for hf in range(2)]
        for h in range(NH):
            pb = h % 4
            s = sl[h // 4]
            hh = h % 4
            for k in range(4):
                S.op("pe", lambda e, k=k, s=s, hh=hh, pb=pb: e.matmul(psum[pb][:, 0:W], c.ring[:, s, k * 1024 + hh * 256:k * 1024 + hh * 256 + 128],
                                                                    c.act[:, k, 0:W], start=(k == 0), stop=(k == 3)),
                     [c.ringB[s], c.actB[k]], [PB[pb]], inc=(k == 3))
            copy_store(c, 128, W, pb, lambda h=h: kT_no[h, :, kc0:kc0 + W])
        for j in range(nblk):
            blk = 0 if t < 0 else 1 + t * 4 + j
            for half in range(2):
                pb = (2 * j + half) % 4
                s = sl[half]
                for k in range(4):
                    S.op("pe", lambda e, k=k, j=j, s=s, pb=pb: e.matmul(
                        psum[pb][0:rows, 0:512].rearrange("p (h c) -> p h c", h=4), c.act[:, k, j * 128:j * 128 + rows],
                        c.ring[:, s, k * 1024:(k + 1) * 1024].rearrange("p (h c) -> p h c", h=4)[:, :, 128:256], start=(k == 0), stop=(k == 3)),
                        [c.ringB[s], c.actB[k]], [PB[pb]], inc=(k == 3))
                copy_store(c, rows, 512, pb, lambda blk=blk, half=half: v_ml[blk, 0:rows, half * 512:(half + 1) * 512])

    with ExitStack() as ph:
        c = alloc_main(ph)
        cs = ph.enter_context(sbt("cs", [64, 2, 512], F32))
        if STOP < 1:
            NT = -1
        csB = Buf("cs")
        csD = S.dsem("cs")
        import os
        tl = [int(v) for v in os.environ["A1T"].split(",")] if "A1T" in os.environ else list(range(-1, NT))
        for t in tl:
            tileA1(c, t, cs, csB, csD)
        S.barrier()

    def tileA2(c, u, wuq, B_wuq, cq, cqB, cqn, cqnB, qr, qrB, cs, csB, csD):
        W = 512
        oc = u * 512
        S.dma("pool", [lambda e: e.dma_start(out=c.xt[:], in_=h1o[:, :, oc:oc + 512].rearrange("k p w -> p k w"))], [], c.xtB, c.xtD)
        S.dma("pool", [lambda e: e.dma_start(out=cs[:, 0, :], in_=cosq[:, oc:oc + 512]),
                       lambda e: e.dma_start(out=cs[:, 1, :], in_=sinq[:, oc:oc + 512])], [], [csB], csD)
        norm_stats(c, c.xt, c.xtB, KT, W, 7, D)
        normalize(c, c.xt, c.xtB, c.xn, c.xnB, KT, W, G_MIXPRE)
        proj_fm(c, W, C_DQ, 4, lambda h, pb: copy_store(c, 128, W, pb, lambda: qT_da[h, :, oc:oc + 512]))
        proj_fm(c, W, C_CQ, 3, lambda j, pb: S.op("dve", lambda e: e.tensor_copy(out=cq[:, j, :], in_=psum[pb][:, 0:W]), [PB[pb]], [cqB[j]]))

        def gate_consume(j, pb):
            def ev(i):
                S.op("act", lambda e: e.activation(out=c.stg[:, i, 0:W], in_=psum[pb][:, 0:W], func=AF.Sigmoid, bias=gsb[:, G_BG + j:G_BG + j + 1]),
                     [PB[pb], B_const], [c.stgB[i]])
            stage_store(c, 128, W, ev, None, lambda: gat[j, :, oc:oc + 512])
        proj_fm(c, W, C_G, 16, gate_consume)
        norm_stats(c, cq, cqB, 6, W, 7, 768)
        normalize(c, cq, cqB, cqn, cqnB, 6, W, G_QN)
        for h in range(NH):
            pb = h % 4
            for k in range(6):
                S.op("pe", lambda e, k=k, h=h, pb=pb: e.matmul(psum[pb][:, 0:W], wuq[:, k, h * 192:h * 192 + 128], cqn[:, k, :], start=(k == 0), stop=(k == 5)),
                     [B_wuq, cqnB[k]], [PB[pb]], inc=(k == 5))
            copy_store(c, 128, W, pb, lambda h=h: qT_no[h, :, oc:oc + 512])
            pr = 4 + (h % 2) * 2
            for k in range(6):
                S.op("pe", lambda e, k=k, h=h, pr=pr: e.matmul(psum[pr][0:64, 0:W], wuq[:, k, h * 192 + 128:h * 192 + 192], cqn[:, k, :], start=(k == 0), stop=(k == 5)),
                     [B_wuq, cqnB[k]], [PB[pr]], inc=(k == 5))
            S.op("dve", lambda e, pr=pr: e.tensor_copy(out=qr[:, :], in_=psum[pr][0:64, 0:W]), [PB[pr]], [qrB])
            rope_store(c, W, qr[:, :], qrB, pr + 1, cs, csB, lambda h=h: qT_ro[h, :, oc:oc + 512])

    with ExitStack() as ph:
        c = alloc_main(ph, need_ffn=False)
        wuq = ph.enter_context(sbt("wuq", [128, 6, 1536], BF16))
        B_wuq = Buf("wuq")
        cq = ph.enter_context(sbt("cq", [128, 6, 512], F32))
        cqB = [Buf("cq%d" % k) for k in range(6)]
        cqn = ph.enter_context(sbt("cqn", [128, 6, 512], BF16))
        cqnB = [Buf("cqn%d" % k) for k in range(6)]
        qr = ph.enter_context(sbt("qr", [64, 512], F32))
        qrB = Buf("qr")
        cs = ph.enter_context(sbt("cs2", [64, 2, 512], F32))
        csB = Buf("cs")
        csD = S.dsem("cs2")
        S.dma("sp", [lambda e: e.dma_start(out=wuq[:], in_=w_b["wuq"].rearrange("(k p) c -> p k c", p=128))], [], [B_wuq], S.dsem("wuq"))
        for u in range(NOT_ if STOP >= 2 else 0):
            tileA2(c, u, wuq, B_wuq, cq, cqB, cqn, cqnB, qr, qrB, cs, csB, csD)
        S.barrier()

    with ExitStack() as ph:
        bsel = ph.enter_context(sbt("bsel", [128, 9, NSLOT, 128], BF16))
        B_bsel = Buf("bsel")
        bm = ph.enter_context(sbt("bm", [16, 9, 128], BF16))
        B_bm = Buf("bm")
        with ExitStack() as p2:
            taug = p2.enter_context(sbt("taug", [33, 16], F32))
            ohs = p2.enter_context(sbt("ohs", [33, NSLOT * 256], F32))
            lh = p2.enter_context(sbt("lh", [33, 128], F32))
            tv = p2.enter_context(sbt("tv", [128, NSLOT * 256], F32))
            bf = p2.enter_context(sbt("bf", [128, NSLOT, 128], F32))
            bmf = p2.enter_context(sbt("bmf", [16, 128], F32))
            B_t = Buf("taug")
            B_lh = Buf("lh")
            B_tv = Buf("tv")
            B_bf = Buf("bf")
            B_toep = Buf("toep")
            B_bmf = Buf("bmf")
            d_t = S.dsem("toep")
            d_t2 = S.dsem("toep2")
            d_t3 = S.dsem("toep3")
            S.op("dve", lambda e: e.memset(taug[:], 0.0), [], [B_t])
            S.op("dve", lambda e: e.memset(taug[32:33, :], NEG), [B_t], [B_t])
            S.dma("sp", [lambda e: e.dma_start(out=taug[0:32, 0:8], in_=tab), lambda e: e.dma_start(out=ohs[:], in_=oh)], [B_t], [B_t], d_t)

            def mk_bias(h):
                S.op("dve", lambda e: e.memset(lh[:], 1.0), [], [B_lh])
                S.op("dve", lambda e: e.tensor_scalar(out=lh[:], in0=lh[:], scalar1=taug[:, h:h + 1], scalar2=None, op0=ALU.mult), [B_t, B_lh], [B_lh])
                ncol = NSLOT * 256
                for c0 in range(0, ncol, 512):
                    c1 = min(ncol, c0 + 512)
                    pb = (c0 // 512) % 4
                    S.op("pe", lambda e, c0=c0, c1=c1, pb=pb: e.matmul(psum[pb][:, 0:c1 - c0], lh[:], ohs[:, c0:c1], start=True, stop=True), [B_lh, B_t], [PB[pb]])
                    S.op("dve", lambda e, c0=c0, c1=c1, pb=pb: e.tensor_scalar(out=tv[:, c0:c1], in0=psum[pb][:, 0:c1 - c0], scalar1=8.0, scalar2=None, op0=ALU.mult),
                         [PB[pb]], [B_tv])
                S.dma("sp", [lambda e: e.dma_start(out=toep[h * NSLOT:(h + 1) * NSLOT, :, :].rearrange("s p c -> p s c"),
                                                   in_=tv[:].rearrange("p (s c) -> p s c", s=NSLOT))], [B_tv], [B_toep], d_t)
                S.dma("sp", [lambda e, sl=sl: e.dma_start(
                    out=bf[:, sl, :], in_=bass.AP(tensor=toep.tensor, offset=(h * NSLOT + sl) * 128 * 256 + 127, ap=[[255, 128], [1, 128]]))
                    for sl in range(NSLOT)], [B_toep], [B_bf], d_t2)
                S.op("dve", lambda e: e.tensor_copy(out=bsel[:, h, :, :], in_=bf[:]), [B_bf], [B_bsel])
                S.dma("sp", [lambda e: e.dma_start(
                    out=bmf[:, :], in_=bass.AP(tensor=toep.tensor, offset=(h * NSLOT + 8) * 128 * 256 + 127 + 112 * 255, ap=[[255, 16], [1, 128]]))],
                    [B_toep], [B_bmf], d_t3)
                S.op("dve", lambda e: e.tensor_copy(out=bm[:, h, :], in_=bmf[:, :]), [B_bmf], [B_bm])
            for h in range(9 if STOP >= 3 else 0):
                mk_bias(h)
            S.barrier()

        kT = ph.enter_context(sbt("kTs", [128, 1, NK], BF16))
        vv = ph.enter_context(sbt("vvs", [128, 1, NB, 128], BF16))
        qq = ph.enter_context(sbt("qqs", [128, 1, NOWN], BF16))
        kTh = ph.enter_context(sbt("kTh", [64, 1, 2, NK], BF16))
        qqh = ph.enter_context(sbt("qqh", [64, 1, 2, NOWN], BF16))
        kro = ph.enter_context(sbt("kro", [64, NK], BF16))
        qro = ph.enter_context(sbt("qro", [64, 1, NOWN], BF16))
        B_kv = [Buf("kv0"), Buf("kv1")]
        D_kv = [S.dsem("kv"), S.dsem("kv")]
        B_kro = Buf("kro")
        pt = ph.enter_context(sbt("pt", [128, 4, 512], BF16))
        ptB = [Buf("pt%d" % i) for i in range(4)]
        fin = ph.enter_context(sbt("fin", [128, 4, 512], F32))
        finB = [Buf("fin%d" % i) for i in range(4)]
        sqf = ph.enter_context(sbt("sqf", [128, 512], BF16))
        sqfB = Buf("sqf")
        ystg = ph.enter_context(sbt("ystg", [128, 2, 512], BF16))
        ystgB = [Buf("y0"), Buf("y1")]
        ystgD = [S.dsem("y"), S.dsem("y")]
        S.dma("sp", [lambda e: e.dma_start(out=kro[:], in_=kT_ro)], [], [B_kro], S.dsem("kro"))
        NQT = NOWN // 512
        ycnt = [0]

        def attn_tile(mode, h, bi, tq):
            nS = 2 if mode == "da" else 1
            bh = h if mode == "da" else 8
            scale = 0.125 if mode == "da" else 192.0 ** -0.5
            kbs = [(0, 0, 16, -1)]
            for g in range(0, 4 * tq + 4):
                for i in range(4):
                    kbs.append((1 + 4 * g + i, 16 + (4 * g + i) * 128, 128, g))
            if mode == "da":
                sbank = [[0, 1], [2, 3]]
                obank = [4, 5]
                lbank = [6, 7]
            else:
                sbank = [[0], [1]]
                obank = [4]
                lbank = [6]
            kbs = kbs[:int(os.environ.get("ATT_STEPS", 10000))]
            nst = len(kbs)
            qe = tq * 512 + 512

            def qk(step):
                blk, kc, nk, g = kbs[step]
                par = step % 2
                m0 = max(g, 4 * tq)
                c0 = (m0 - 4 * tq) * 128
                qc0 = tq * 512 + c0
                extra = []
                if g < 0:
                    if tq == 0:
                        extra.append(("m", 0, 0))
                else:
                    for m in range(m0, 4 * tq + 4):
                        if g in (m - 1, m):
                            extra.append(("b", (m - 4 * tq) * 128, 4 * (g - m + 1) + (blk - 1 - 4 * g)))
                nex = len(extra)
                for si in range(nS):
                    pb = sbank[par][si]
                    if mode == "da":
                        p0 = 64 * si
                        S.op("pe", lambda e, si=si, pb=pb: e.matmul(psum[pb][0:nk, c0:512], kTh[:, bi, si, kc:kc + nk], qqh[:, bi, si, qc0:qe],
                                                                  start=True, stop=(nex == 0)), [B_kv[bi]], [PB[pb]], inc=(nex == 0))
                    else:
                        S.op("pe", lambda e, pb=pb: e.matmul(psum[pb][0:nk, c0:512], kT[:, bi, kc:kc + nk], qq[:, bi, qc0:qe], start=True, stop=False),
                             [B_kv[bi]], [PB[pb]], inc=False)
                        S.op("pe", lambda e, pb=pb: e.matmul(psum[pb][0:nk, c0:512], kro[:, kc:kc + nk], qro[:, bi, qc0:qe], start=False, stop=(nex == 0)),
                             [B_kv[bi], B_kro], [PB[pb]], inc=(nex == 0))
                    for xi, ex in enumerate(extra):
                        last = xi == nex - 1
                        if ex[0] == "m":
                            S.op("pe", lambda e, pb=pb, last=last: e.matmul(psum[pb][0:16, 0:128], ident_b[0:16, 0:16], bm[:, bh, :], start=False, stop=last),
                                 [B_bm, B_const], [PB[pb]], inc=last)
                        else:
                            _, cc, sl = ex
                            S.op("pe", lambda e, pb=pb, last=last, cc=cc, sl=sl: e.matmul(psum[pb][:, cc:cc + 128], ident_b[:], bsel[:, bh, sl, :], start=False, stop=last),
                                 [B_bsel, B_const], [PB[pb]], inc=last)
                    pi = par * 2 + si
                    S.op("act", lambda e, pb=pb, pi=pi: e.activation(out=pt[0:nk, pi, c0:512], in_=psum[pb][0:nk, c0:512], func=AF.Exp, scale=scale),
                         [PB[pb]], [ptB[pi]])
                return (blk, nk, c0, par)

            def pv(info, first, lastk):
                blk, nk, c0, par = info
                for si in range(nS):
                    pi = par * 2 + si
                    S.op("pe", lambda e, si=si, pi=pi: e.matmul(psum[obank[si]][:, c0:512], vv[0:nk, bi, blk, :], pt[0:nk, pi, c0:512], start=first, stop=lastk),
                         [B_kv[bi], ptB[pi]], [PB[obank[si]]], inc=True)
                    S.op("pe", lambda e, si=si, pi=pi: e.matmul(psum[lbank[si]][:, c0:512], ones_b[0:nk, :], pt[0:nk, pi, c0:512], start=first, stop=lastk),
                         [B_const, ptB[pi]], [PB[lbank[si]]], inc=True)

            pend = None
            for step in range(nst + 1):
                cur = qk(step) if step < nst else None
                if pend is not None:
                    pv(pend, step == 1, step == nst)
                pend = cur
            oc = tq * 512
            for si in range(nS):
                S.op("dve", lambda e, si=si: e.reciprocal(out=fin[:, si, :], in_=psum[lbank[si]][:, :]), [PB[lbank[si]]], [finB[si]])
                S.op("dve", lambda e, si=si: e.tensor_tensor(out=fin[:, si, :], in0=psum[obank[si]][:, :], in1=fin[:, si, :], op=ALU.mult),
                     [PB[obank[si]], finB[si]], [finB[si]])
            yi = ycnt[0] % 2
            ycnt[0] += 1
            if mode == "da":
                S.op("dve", lambda e: e.scalar_tensor_tensor(out=fin[:, 2, :], in0=fin[:, 1, :], scalar=neglam[:], in1=fin[:, 0, :], op0=ALU.mult, op1=ALU.add),
                     [finB[0], finB[1], B_const], [finB[2]])
                S.op("act", lambda e: e.activation(out=sqf[:], in_=fin[:, 2, :], func=AF.Square), [finB[2]], [sqfB])
                S.op("pe", lambda e: e.matmul(psum[0][:, :], ones_b[:], sqf[:], start=True, stop=True), [sqfB, B_const], [PB[0]])
                S.op("act", lambda e: e.activation(out=fin[:, 3, :], in_=psum[0][:, :], func=AF.Sqrt, scale=1.0 / 128, bias=epsb[:]), [PB[0], B_const], [finB[3]])
                S.op("dve", lambda e: e.reciprocal(out=fin[:, 3, :], in_=fin[:, 3, :]), [finB[3]], [finB[3]])
                S.op("dve", lambda e: e.scalar_tensor_tensor(out=ystg[:, yi, :], in0=fin[:, 2, :], scalar=subg8[:], in1=fin[:, 3, :], op0=ALU.mult, op1=ALU.mult),
                     [finB[2], finB[3], B_const], [ystgB[yi]])
                S.dma("pool", [lambda e: e.dma_start(out=y_da[h, :, oc:oc + 512], in_=ystg[:, yi, :])], [ystgB[yi]], [], ystgD[yi])
            else:
                S.op("act", lambda e: e.activation(out=ystg[:, yi, :], in_=fin[:, 0, :], func=AF.Copy), [finB[0]], [ystgB[yi]])
                S.dma("pool", [lambda e: e.dma_start(out=y_ml[h, :, oc:oc + 512], in_=ystg[:, yi, :])], [ystgB[yi]], [], ystgD[yi])

        def attn_head(mode, h, bi):
            if mode == "da":
                fns = [lambda e: e.dma_start(out=kTh[:, bi, 0, :], in_=kT_da[h, 0:64, :]),
                       lambda e: e.dma_start(out=kTh[:, bi, 1, :], in_=kT_da[h, 64:128, :]),
                       lambda e: e.dma_start(out=vv[:, bi, :, :], in_=v_da[:, :, h * 128:(h + 1) * 128].rearrange("b p c -> p b c")),
                       lambda e: e.dma_start(out=qqh[:, bi, 0, :], in_=qT_da[h, 0:64, :]),
                       lambda e: e.dma_start(out=qqh[:, bi, 1, :], in_=qT_da[h, 64:128, :])]
            else:
                fns = [lambda e: e.dma_start(out=kT[:, bi, :], in_=kT_no[h]),
                       lambda e: e.dma_start(out=vv[:, bi, :, :], in_=v_ml[:, :, h * 128:(h + 1) * 128].rearrange("b p c -> p b c")),
                       lambda e: e.dma_start(out=qq[:, bi, :], in_=qT_no[h]),
                       lambda e: e.dma_start(out=qro[:, bi, :], in_=qT_ro[h])]
            S.dma("sp", fns, [], [B_kv[bi]], D_kv[bi])
            for tq in range(int(os.environ.get("ATT_TQ", NQT))):
                attn_tile(mode, h, bi, tq)

        hc = 0
        import os
        for mode in os.environ.get("ATT_MODES", "da,mla").split(","):
            for h in range(int(os.environ.get("ATT_H", NH)) if STOP >= 4 else 0):
                attn_head(mode, h, 0)
                hc += 1
        S.barrier()

    def tileC(c, u, ya, ym, yB, yD, gt, gtB, gtD, d_out):
        W = 512
        oc = u * 512
        S.dma("pool", [lambda e: e.dma_start(out=c.xt[:], in_=h1o[:, :, oc:oc + 512].rearrange("k p w -> p k w"))], [], c.xtB, c.xtD)
        S.dma("pool", [lambda e: e.dma_start(out=ya[:], in_=y_da[:, :, oc:oc + 512].rearrange("k p w -> p k w")),
                       lambda e: e.dma_start(out=ym[:], in_=y_ml[:, :, oc:oc + 512].rearrange("k p w -> p k w"))], [], [yB], yD)
        for o in range(KT):
            gi = o % 2
            S.dma("pool", [lambda e, o=o, gi=gi: e.dma_start(out=gt[:, gi, 0, :], in_=gat[o, :, oc:oc + 512]),
                           lambda e, o=o, gi=gi: e.dma_start(out=gt[:, gi, 1, :], in_=gat[16 + o, :, oc:oc + 512])], [], [gtB[gi]], gtD[gi])
            s1 = wload(c, "wbd", 0, 8, o * 128, 128)
            s2 = wload(c, "wbm", 0, 8, o * 128, 128)
            pa = 2 * (o % 2)
            pm = pa + 1
            for k in range(8):
                S.op("pe", lambda e, s1=s1, k=k, pa=pa: e.matmul(psum[pa][:, :], c.ring[:, s1, k * 128:(k + 1) * 128], ya[:, k, :], start=(k == 0), stop=(k == 7)),
                     [c.ringB[s1], yB], [PB[pa]], inc=(k == 7))
            for k in range(8):
                S.op("pe", lambda e, s2=s2, k=k, pm=pm: e.matmul(psum[pm][:, :], c.ring[:, s2, k * 128:(k + 1) * 128], ym[:, k, :], start=(k == 0), stop=(k == 7)),
                     [c.ringB[s2], yB], [PB[pm]], inc=(k == 7))
            t = c.tp % 2
            c.tp += 1
            S.op("dve", lambda e, t=t, pa=pa, gi=gi: e.tensor_tensor(out=c.tmp[:, t, :], in0=psum[pa][:, :], in1=gt[:, gi, 0, :], op=ALU.mult),
                 [PB[pa], gtB[gi]], [c.tmpB[t]])
            t2 = c.tp % 2
            c.tp += 1
            S.op("dve", lambda e, t2=t2, pm=pm, gi=gi: e.tensor_tensor(out=c.tmp[:, t2, :], in0=psum[pm][:, :], in1=gt[:, gi, 1, :], op=ALU.mult),
                 [PB[pm], gtB[gi]], [c.tmpB[t2]])
            S.op("dve", lambda e, t=t, t2=t2, o=o: e.tensor_tensor(out=c.xn[:, o, :], in0=c.tmp[:, t, :], in1=c.tmp[:, t2, :], op=ALU.add),
                 [c.tmpB[t], c.tmpB[t2]], [c.xnB[o]])
        down_like(c, "wo", KT, c.xn, c.xnB, W, [4, 5], 6)
        residual(c, W, G_MIXPOST, False)
        ffn(c, W, "f2", G_F2PRE, G_F2POST)
        S.dma("pool", [lambda e: e.dma_start(out=outT[:, oc:oc + 512].rearrange("(k p) w -> p k w", p=128), in_=c.xt[:])], c.xtB, [], d_out)

    with ExitStack() as ph:
        c = alloc_main(ph, need_stg=False)
        ya = ph.enter_context(sbt("ya", [128, 8, 512], BF16))
        ym = ph.enter_context(sbt("ym", [128, 8, 512], BF16))
        yB = Buf("y")
        yD = S.dsem("yload")
        gt = ph.enter_context(sbt("gt", [128, 2, 2, 512], BF16))
        gtB = [Buf("gt0"), Buf("gt1")]
        gtD = [S.dsem("gt"), S.dsem("gt")]
        d_out = S.dsem("out")
        for u in range(NOT_ if STOP >= 5 else 0):
            tileC(c, u, ya, ym, yB, yD, gt, gtB, gtD, d_out)
        S.barrier()

    with nc.Block() as block:
        @block.tensor
        def _(e):
            for f in S.q["pe"].prog:
                f(e)

        @block.scalar
        def _(e):
            for f in S.q["act"].prog:
                f(e)

        @block.vector
        def _(e):
            for f in S.q["dve"].prog:
                f(e)

        @block.gpsimd
        def _(e):
            for f in S.q["pool"].prog:
                f(e)

        @block.sync
        def _(e):
            for f in S.q["sp"].prog:
                f(e)
    es.close()
    build.last_sched = S
    return nc


def _bucket(rel):
    n = np.maximum(rel, 0)
    n_f = np.maximum(n, 1).astype(np.float32)
    large = 16 + (np.log(n_f / np.float32(16)) / np.float32(math.log(128 / 16)) * np.float32(16)).astype(np.int32)
    large = np.minimum(large, 31)
    return np.where(n < 16, n, large)


def _bucket_jax(rel):
    import jax.numpy as jnp
    n = jnp.maximum(jnp.asarray(rel, jnp.int32), 0)
    n_f = jnp.maximum(n, 1).astype(jnp.float32)
    large = 16 + (jnp.log(n_f / 16) / math.log(128 / 16) * 16).astype(jnp.int32)
    large = jnp.minimum(large, 31)
    return np.asarray(jnp.where(n < 16, n, large))


def host_prep(inputs, SEQ, DFF):
    x = np.asarray(inputs["x"], np.float32)
    B = x.shape[0]
    NT = SEQ // 512
    NK = 16 + SEQ
    maps = []
    half = 32
    inv = (10000.0 ** (-np.arange(half, dtype=np.float32) * 2.0 / 64)).astype(np.float32)
    g = lambda n: np.asarray(inputs[n], np.float32)[0]
    gains = np.concatenate(
        [g(n).reshape(-1, 128).T for n in ("ffn1_pre_g", "ffn1_post_g", "mix_pre_g", "mix_post_g", "ffn2_pre_g", "ffn2_post_g")]
        + [g("b_gate").reshape(32, 128).T, g("mla_q_norm_g").reshape(6, 128).T, g("mla_kv_norm_g").reshape(4, 128).T,
           g("da_sub_g").reshape(1, 128).T], axis=1).astype(np.float32)
    lamv = np.concatenate([g("da_lambda_q1"), g("da_lambda_k1"), g("da_lambda_q2"), g("da_lambda_k2")])[None, :].repeat(128, 0).astype(np.float32)
    rot = np.zeros((64, 64), np.float32)
    for i in range(32):
        rot[32 + i, i] = -1.0
        rot[i, 32 + i] = 1.0
    ident = np.eye(128, dtype=np.float32)
    wmap = {"f1g": "ffn1_w_gate", "f1u": "ffn1_w_up", "f1d": "ffn1_w_down", "f2g": "ffn2_w_gate", "f2u": "ffn2_w_up", "f2d": "ffn2_w_down",
            "win": "w_in", "wuq": "mla_w_uq", "wukv": "mla_w_ukv", "wbd": "w_branch_da", "wbm": "w_branch_mla", "wo": "w_out"}
    import os
    shared = {"wsmall": (np.random.RandomState(0).randn(2048, 2048) / 45.0).astype(np.float32)} if "DBG_NOW" in os.environ else {"w_" + k: np.ascontiguousarray(np.asarray(inputs[v], np.float32)[0]) for k, v in wmap.items()}
    shared.update({"gains": gains, "lamv": lamv, "tab": np.asarray(inputs["rel_bias_table"], np.float32), "ident": ident, "rot": rot,
                   "metaT": np.ascontiguousarray(np.asarray(inputs["meta_tokens"], np.float32).T)})
    deltas = np.arange(256) - 127
    for core in range(8):
        b, c = core // 4, core % 4
        perm = [c] + [i for i in range(4) if i != c]
        order = np.concatenate([np.arange((4 * t + perm[i]) * 128, (4 * t + perm[i]) * 128 + 128) for t in range(NT) for i in range(4)])
        xT = np.ascontiguousarray(x[b][order].T)
        kpos = np.concatenate([np.arange(16), 16 + order]).astype(np.float32)
        ang = kpos[None, :] * inv[:, None]
        cosk = np.concatenate([np.cos(ang), np.cos(ang)], 0).astype(np.float32)
        sink = np.concatenate([np.sin(ang), np.sin(ang)], 0).astype(np.float32)
        own = np.concatenate([np.arange(16 + t * 512, 16 + t * 512 + 128) for t in range(NT)])
        cosq, sinq = np.ascontiguousarray(cosk[:, own]), np.ascontiguousarray(sink[:, own])
        oh = np.zeros((33, NSLOT * 256), np.float32)
        for sl in range(NSLOT):
            if sl < 8:
                gg, i = sl // 4, sl % 4
                dist = (4 + c) - (4 * gg + perm[i])
            else:
                dist = c + 1
            rel = 128 * dist + deltas
            bk = _bucket(rel)
            for j in range(255):
                if rel[j] < 0:
                    oh[32, sl * 256 + j] = 1.0
                else:
                    oh[bk[j], sl * 256 + j] += 1.0
                    oh[31, sl * 256 + j] -= 1.0
        m = dict(shared)
        m.update({"xT": xT, "cosk": cosk, "sink": sink, "cosq": cosq, "sinq": sinq, "oh": oh})
        maps.append(m)
    return maps


def host_post(results, B, SEQ):
    NT = SEQ // 512
    out = np.zeros((B, SEQ, D), np.float32)
    for core in range(8):
        b, c = core // 4, core % 4
        oT = np.asarray(results[core]["outT"])
        o = oT.T.reshape(NT, 128, D)
        for m in range(NT):
            out[b, (4 * m + c) * 128:(4 * m + c + 1) * 128] = o[m]
    return out


_CACHE = {}


def run(inputs, SEQ, DFF, STOP=99):
    key = (SEQ, DFF, STOP)
    if key not in _CACHE:
        _CACHE[key] = build(SEQ, DFF, STOP)
    nc = _CACHE[key]
    maps = host_prep(inputs, SEQ, DFF)
    res = run_bass_kernel_spmd(nc, maps, core_ids=list(range(8)))
    run.last_results = res.results
    return host_post(res.results, 2, SEQ)


def kernel(**inputs):
    return run(inputs, 8192, 5632)
```

```python
import math
from contextlib import ExitStack
import numpy as np
import concourse.bass as bass
import concourse.mybir as mybir
from concourse.bass_utils import run_bass_kernel_spmd

F32 = mybir.dt.float32
BF16 = mybir.dt.bfloat16
AF = mybir.ActivationFunctionType
ALU = mybir.AluOpType

D = 2048
KT = D // 128
NH = 8
EPS = 1e-6
NEG = -30000.0
NSLOT = 9
INW = 8512
C_DQ, C_DK, C_DV, C_CQ, C_CKV, C_KR, C_G = 0, 1024, 2048, 3072, 3840, 4352, 4416


class Buf:
    __slots__ = ("w", "r", "name", "excl")

    def __init__(self, name="", excl=False):
        self.w = None
        self.r = {}
        self.name = name
        self.excl = excl


class Q:
    def __init__(self, name, sem, selfwait):
        self.name = name
        self.sem = sem
        self.cnt = 0
        self.seen = {}
        self.selfwait = selfwait
        self.prog = []
        self.trace = []


class DSem:
    def __init__(self, sem):
        self.sem = sem
        self.cnt = 0


class Sched:
    def __init__(self, nc, es):
        self.nc = nc
        self.es = es
        self.q = {}
        for name, sw in (("pe", False), ("act", True), ("dve", True), ("pool", True), ("sp", False)):
            self.q[name] = Q(name, es.enter_context(nc.semaphore("q_" + name)), sw)
        self.dsems = []
        self.tag = ""

    def simulate(self):
        sems = {}
        pc = {n: 0 for n in self.q}
        prog = True
        while prog:
            prog = False
            for n, q in self.q.items():
                while pc[n] < len(q.trace):
                    kind, sid, val, nm = q.trace[pc[n]]
                    if kind == "w":
                        if sems.get(sid, 0) < val:
                            break
                    elif kind == "i":
                        sems[sid] = sems.get(sid, 0) + val
                    pc[n] += 1
                    prog = True
        stuck = {n: (pc[n], len(q.trace)) for n, q in self.q.items() if pc[n] < len(q.trace)}
        for n in stuck:
            q = self.q[n]
            kind, sid, val, nm = q.trace[pc[n]]
            prev = [t[3] for t in q.trace[max(0, pc[n] - 3):pc[n]] if t[0] != "w"]
            nxt = [t[3] for t in q.trace[pc[n]:pc[n] + 6] if t[0] != "w"]
            print("STUCK", n, pc[n], "/", len(q.trace), "waits", nm, "val", val, "have", sems.get(sid, 0), "prev", prev, "next", nxt)
        return not stuck

    def dsem(self, name):
        d = DSem(self.es.enter_context(self.nc.semaphore("d_" + name + str(len(self.dsems)))))
        self.dsems.append(d)
        return d

    def _wait(self, q, ev):
        kind, key, val = ev
        if kind == "eng" and key is q and not q.selfwait:
            return
        k = id(key)
        if q.seen.get(k, 0) >= val:
            return
        q.seen[k] = val
        sem = key.sem
        q.prog.append(lambda e, sem=sem, val=val: e.wait_ge(sem, val))
        q.trace.append(("w", id(key), val, getattr(key, "name", "dsem")))

    def _sync(self, q, reads, writes):
        for b in reads:
            if b.w is not None:
                self._wait(q, b.w)
        for b in writes:
            if b.w is not None:
                self._wait(q, b.w)
            for r in b.r.values():
                self._wait(q, r)

    @staticmethod
    def _commit(ev, reads, writes):
        for b in writes:
            b.w = ev
            b.r = {}
        for b in reads:
            k = id(ev[1])
            o = b.r.get(k)
            if o is None or o[2] < ev[2]:
                b.r[k] = ev

    def op(self, qn, fn, reads=(), writes=(), inc=True):
        q = self.q[qn]
        ex = [b for b in reads if b.excl]
        if ex:
            reads = [b for b in reads if not b.excl]
            writes = list(writes) + ex
        self._sync(q, reads, writes)
        if inc:
            q.cnt += 1
            sem = q.sem
            q.prog.append(lambda e, fn=fn, sem=sem: fn(e).then_inc(sem, 1))
            q.trace.append(("i", id(q), 1, self.tag))
            ev = ("eng", q, q.cnt)
        else:
            q.prog.append(lambda e, fn=fn: fn(e))
            q.trace.append(("n", 0, 0, self.tag))
            ev = ("eng", q, q.cnt + 1)
        self._commit(ev, reads, writes)

    def dma(self, qn, fns, reads, writes, ds):
        q = self.q[qn]
        self._sync(q, reads, writes)
        for fn in fns:
            ds.cnt += 16
            sem = ds.sem
            q.prog.append(lambda e, fn=fn, sem=sem: fn(e).then_inc(sem, 16))
            q.trace.append(("i", id(ds), 16, self.tag))
        self._commit(("dma", ds, ds.cnt), reads, writes)

    def barrier(self, skip=()):
        sp = self.q["sp"]
        for d in self.dsems:
            if d.cnt > 0 and d not in skip:
                self._wait(sp, ("dma", d, d.cnt))
        for n in ("pe", "act", "dve", "pool"):
            if self.q[n].cnt > 0:
                self._wait(sp, ("eng", self.q[n], self.q[n].cnt))
        sp.cnt += 1
        sem = sp.sem
        sp.prog.append(lambda e, sem=sem: e.nop().then_inc(sem, 1))
        sp.trace.append(("i", id(sp), 1, "barrier"))
        for n in ("pe", "act", "dve", "pool"):
            self._wait(self.q[n], ("eng", sp, sp.cnt))


def build(SEQ, DFF, STOP=99):
    NT = SEQ // 512
    NOWN = SEQ // 4
    NOT_ = NOWN // 512
    NK = 16 + SEQ
    NB = 1 + SEQ // 128
    FT = DFF // 128
    FH = FT // 2

    nc = bass.Bass("TRN2", target_bir_lowering=False)
    es = ExitStack()
    S = Sched(nc, es)
    _cnt = [0]

    def sbt(name, shape, dt):
        _cnt[0] += 1
        return nc.sbuf_tensor("%s_%d" % (name, _cnt[0]), shape, dt)

    def din(name, shape, dt=F32):
        return nc.dram_tensor(name, list(shape), dt, kind="ExternalInput").ap()

    def dscr(name, shape, dt=BF16):
        import os
        if "DBG_OUT" in os.environ and not name.startswith("wb_") and not name.startswith("wt_") and name != "toep":
            return nc.dram_tensor(name, list(shape), dt, kind="ExternalOutput").ap()
        return nc.dram_tensor(name, list(shape), dt).ap()

    xT = din("xT", [D, SEQ])
    metaT = din("metaT", [D, 16])
    w_f = {}
    wshape = {"f1g": (D, DFF), "f1u": (D, DFF), "f1d": (DFF, D), "f2g": (D, DFF), "f2u": (D, DFF), "f2d": (DFF, D),
              "win": (D, INW), "wuq": (768, 1536), "wukv": (512, 2048), "wbd": (1024, D), "wbm": (1024, D), "wo": (D, D)}
    import os
    NOW = "DBG_NOW" in os.environ
    for n, shp in wshape.items():
        if not NOW:
            w_f[n] = din("w_" + n, shp)
    if NOW:
        wsmall = din("wsmall", [2048, 2048])
    gains = din("gains", [128, 6 * KT + 32 + 6 + 4 + 1])
    lamv = din("lamv", [128, 4 * 64])
    tab = din("tab", [32, 8])
    oh = din("oh", [33, NSLOT * 256])
    ident_in = din("ident", [128, 128])
    rot_in = din("rot", [64, 64])
    cosk = din("cosk", [64, NK])
    sink = din("sink", [64, NK])
    cosq = din("cosq", [64, NOWN])
    sinq = din("sinq", [64, NOWN])
    outT = nc.dram_tensor("outT", [D, NOWN], F32, kind="ExternalOutput").ap()

    w_b = {n: dscr("wb_" + n, shp) for n, shp in wshape.items()}
    tile_specs = {}

    def _add(w, r0, nk, c0, ncols):
        tile_specs.setdefault(w, []).append((r0, nk, c0, ncols))
    _nhalf = 2 if FT > 32 else 1
    _kh = FT // _nhalf
    USE_TILED = False
    for pre_ in (("f1", "f2") if USE_TILED else ()):
        for f_ in range(FT):
            _add(pre_ + "g", 0, KT, f_ * 128, 128)
            _add(pre_ + "u", 0, KT, f_ * 128, 128)
        for o_ in range(KT):
            for hh_ in range(_nhalf):
                _add(pre_ + "d", hh_ * _kh * 128, _kh, o_ * 128, 128)
    for base_, n_ in (((C_DQ, 4), (C_DK, 4), (C_DV, 4), (C_CQ, 3), (C_CKV, 2), (C_G, 16)) if USE_TILED else ()):
        for i_ in range(n_):
            _add("win", 0, KT, base_ + i_ * 256, 256)
    if USE_TILED:
        _add("win", 0, KT, C_KR, 64)
    for o_ in (range(KT) if USE_TILED else ()):
        _add("wbd", 0, 8, o_ * 128, 128)
        _add("wbm", 0, 8, o_ * 128, 128)
        _add("wo", 0, KT, o_ * 128, 128)
    w_t = {}
    tidx = {}
    for w_, lst_ in tile_specs.items():
        w_t[w_] = dscr("wt_" + w_, [len(lst_), 128, max(nk_ * nc_ for (_, nk_, _, nc_) in lst_)])
        for i_, (r0_, nk_, c0_, nc_) in enumerate(lst_):
            tidx[(w_, r0_, c0_)] = i_
    kT_da = dscr("kT_da", [NH, 128, NK])
    v_da = dscr("v_da", [NB, 128, 1024])
    kT_no = dscr("kT_no", [NH, 128, NK])
    kT_ro = dscr("kT_ro", [64, NK])
    v_ml = dscr("v_ml", [NB, 128, 1024])
    qT_da = dscr("qT_da", [NH, 128, NOWN])
    qT_no = dscr("qT_no", [NH, 128, NOWN])
    qT_ro = dscr("qT_ro", [NH, 64, NOWN])
    gat = dscr("gat", [32, 128, NOWN])
    h1o = dscr("h1o", [KT, 128, NOWN], F32)
    y_da = dscr("y_da", [NH, 128, NOWN])
    y_ml = dscr("y_ml", [NH, 128, NOWN])
    toep = dscr("toep", [9 * NSLOT, 128, 256], F32)

    def sb(name, shape, dt):
        return es.enter_context(sbt(name, list(shape), dt))

    gsb = sb("gsb", [128, 6 * KT + 43], F32)
    ones_b = sb("ones_b", [128, 128], BF16)
    ident_b = sb("ident_b", [128, 128], BF16)
    rot_f = sb("rot_f", [64, 64], F32)
    neglam = sb("neglam", [128, 1], F32)
    subg8 = sb("subg8", [128, 1], F32)
    epsb = sb("epsb", [128, 1], F32)
    B_const = Buf("const")
    psum = [es.enter_context(nc.psum_tensor("ps%d" % i, [128, 512], F32)) for i in range(8)]
    PB = [Buf("ps%d" % i, excl=True) for i in range(8)]
    G_F1PRE, G_F1POST, G_MIXPRE, G_MIXPOST, G_F2PRE, G_F2POST = [i * KT for i in range(6)]
    G_BG = 6 * KT
    G_QN = G_BG + 32
    G_KVN = G_QN + 6
    G_SUBG = G_KVN + 4

    d_const = S.dsem("const")

    S.dma("sp", [lambda e: e.dma_start(out=gsb[:], in_=gains),
                 lambda e: e.dma_start(out=rot_f[:], in_=rot_in)], [], [B_const], d_const)
    S.op("dve", lambda e: e.memset(ones_b[:], 1.0), [], [B_const])
    S.op("dve", lambda e: e.memset(epsb[:], EPS), [], [B_const])
    S.dma("pool", [lambda e: e.dma_start(out=ident_b[:], in_=ident_in)], [], [B_const], S.dsem("ident"))
    d_conv = S.dsem("conv")
    conv_fns = []
    conv_fns2 = []
    FIRST = ("f1g", "f1u", "f1d", "win", "wukv")
    for n, shp in wshape.items():
        rows = shp[0]
        step = 256
        for r0 in range(0, rows, step):
            r1 = min(rows, r0 + step)
            (conv_fns if n in FIRST else conv_fns2).append(
                lambda e, n=n, r0=r0, r1=r1: e.dma_start(out=w_b[n][r0:r1, :], in_=w_f[n][r0:r1, :]))
    if NOW:
        conv_fns = []
        for n, shp in wshape.items():
            for r0 in range(0, shp[0], 256):
                r1 = min(shp[0], r0 + 256)
                for c0 in range(0, shp[1], 2048):
                    c1 = min(shp[1], c0 + 2048)
                    conv_fns.append(lambda e, n=n, r0=r0, r1=r1, c0=c0, c1=c1: e.dma_start(
                        out=w_b[n][r0:r1, c0:c1], in_=wsmall[r0 % 2048:r0 % 2048 + (r1 - r0), 0:c1 - c0]))
    B_wb = Buf("wb")
    B_wt = Buf("wt")
    S.dma("pool", conv_fns, [], [B_wb], d_conv)
    d_conv2 = S.dsem("conv2")
    if not NOW:
        S.dma("pool", conv_fns2, [], [], d_conv2)
    d_tile = S.dsem("tile")
    tile_fns = []
    for w_, lst_ in tile_specs.items():
        for i_, (r0_, nk_, c0_, nc_) in enumerate(lst_):
            tile_fns.append(lambda e, w_=w_, i_=i_, r0_=r0_, nk_=nk_, c0_=c0_, nc_=nc_: e.dma_start(
                out=w_t[w_][i_, :, 0:nk_ * nc_].rearrange("p (k c) -> p k c", k=nk_),
                in_=w_b[w_][r0_:r0_ + nk_ * 128, c0_:c0_ + nc_].rearrange("(k p) c -> p k c", p=128)))
    S.dma("sp", tile_fns, [B_wb], [B_wt], d_tile)

    with ExitStack() as ph:
        lv = ph.enter_context(sbt("lv", [128, 256], F32))
        lt = ph.enter_context(sbt("lt", [128, 128], F32))
        ls = ph.enter_context(sbt("ls", [128, 4], F32))
        B_l = Buf("lam")
        S.dma("sp", [lambda e: e.dma_start(out=lv[:], in_=lamv)], [], [B_l], S.dsem("lamv"))
        S.op("dve", lambda e: e.tensor_tensor(out=lt[:, 0:64], in0=lv[:, 0:64], in1=lv[:, 64:128], op=ALU.mult), [B_l], [B_l])
        S.op("dve", lambda e: e.tensor_tensor(out=lt[:, 64:128], in0=lv[:, 128:192], in1=lv[:, 192:256], op=ALU.mult), [B_l], [B_l])
        S.op("dve", lambda e: e.reduce_sum(out=ls[:, 0:1], in_=lt[:, 0:64], axis=mybir.AxisListType.X), [B_l], [B_l])
        S.op("dve", lambda e: e.reduce_sum(out=ls[:, 1:2], in_=lt[:, 64:128], axis=mybir.AxisListType.X), [B_l], [B_l])
        S.op("act", lambda e: e.activation(out=ls[:, 2:4], in_=ls[:, 0:2], func=AF.Exp), [B_l], [B_l])
        S.op("dve", lambda e: e.scalar_tensor_tensor(out=neglam[:], in0=ls[:, 3:4], scalar=-0.2, in1=ls[:, 2:3],
                                                     op0=ALU.add, op1=ALU.subtract), [B_l], [B_l, B_const])
        S.op("dve", lambda e: e.tensor_scalar(out=subg8[:], in0=gsb[:, G_SUBG:G_SUBG + 1], scalar1=0.8, scalar2=None,
                                              op0=ALU.mult), [B_const], [B_const])
        S.barrier(skip=(d_conv2,))

    NRING = 4

    class Ctx:
        pass

    def alloc_main(ph, need_ffn=True, need_stg=True):
        c = Ctx()
        c.ring = ph.enter_context(sbt("ring", [128, NRING, 4096], BF16))
        c.ringB = [Buf("ring%d" % i) for i in range(NRING)]
        c.ringD = [S.dsem("ring") for _ in range(NRING)]
        c.rp = 0
        c.xt = ph.enter_context(sbt("xt", [128, KT, 512], F32))
        c.xtB = [Buf("xt%d" % k) for k in range(KT)]
        c.xtD = S.dsem("xt")
        c.xsD = S.dsem("xs")
        c.xn = ph.enter_context(sbt("xn", [128, KT, 512], BF16))
        c.xnB = [Buf("xn%d" % k) for k in range(KT)]
        if need_ffn:
            c.act = ph.enter_context(sbt("act", [128, max(FT, 4), 512], BF16))
            c.actB = [Buf("act%d" % k) for k in range(max(FT, 4))]
            c.ft = ph.enter_context(sbt("ft", [128, KT, 512], F32))
            c.ftB = [Buf("ft%d" % k) for k in range(KT)]
        c.sq = ph.enter_context(sbt("sq", [128, 4, 512], BF16))
        c.sqB = [Buf("sq%d" % k) for k in range(4)]
        c.sqp = 0
        c.rstd = ph.enter_context(sbt("rstd", [128, 512], F32))
        c.rstdB = Buf("rstd")
        c.tmp = ph.enter_context(sbt("tmp", [128, 2, 512], F32))
        c.tmpB = [Buf("tmp0"), Buf("tmp1")]
        c.tp = 0
        if need_stg:
            c.stg = ph.enter_context(sbt("stg", [128, 4, 512], BF16))
            c.stgB = [Buf("stg%d" % k) for k in range(4)]
            c.stgD = [S.dsem("stg") for _ in range(4)]
        c.sp_ = 0
        return c

    def ring_load(c, fns_for_slot):
        s = c.rp % NRING
        c.rp += 1
        S.dma("sp", fns_for_slot(s), [], [c.ringB[s]], c.ringD[s])
        return s

    def wload(c, wname, r0, nk, c0, ncols):
        w = w_b[wname]
        if (wname, r0, c0) in tidx:
            ti = tidx[(wname, r0, c0)]

            def fns_t(s):
                return [lambda e: e.dma_start(out=c.ring[:, s, 0:nk * ncols], in_=w_t[wname][ti, :, 0:nk * ncols])]
            return ring_load(c, fns_t)

        def fns(s):
            return [lambda e: e.dma_start(
                out=c.ring[:, s, 0:nk * ncols].rearrange("p (k c) -> p k c", k=nk),
                in_=w[r0:r0 + nk * 128, c0:c0 + ncols].rearrange("(k p) c -> p k c", p=128))]
        return ring_load(c, fns)

    def norm_stats(c, src, srcB, nk, W, pb, dim):
        for k in range(nk):
            i = c.sqp % 4
            c.sqp += 1
            S.op("act", lambda e, k=k, i=i: e.activation(out=c.sq[:, i, 0:W], in_=src[:, k, 0:W], func=AF.Square),
                 [srcB[k]], [c.sqB[i]])
            S.op("pe", lambda e, k=k, i=i: e.matmul(psum[pb][:, 0:W], ones_b[:], c.sq[:, i, 0:W], start=(k == 0), stop=(k == nk - 1)),
                 [c.sqB[i], B_const], [PB[pb]], inc=True)
        S.op("act", lambda e: e.activation(out=c.rstd[:, 0:W], in_=psum[pb][:, 0:W], func=AF.Sqrt, scale=1.0 / dim, bias=epsb[:]),
             [PB[pb], B_const], [c.rstdB])
        S.op("dve", lambda e: e.reciprocal(out=c.rstd[:, 0:W], in_=c.rstd[:, 0:W]), [c.rstdB], [c.rstdB])

    def normalize(c, src, srcB, dst, dstB, nk, W, gcol):
        for k in range(nk):
            S.op("dve", lambda e, k=k: e.scalar_tensor_tensor(out=dst[:, k, 0:W], in0=src[:, k, 0:W], scalar=gsb[:, gcol + k:gcol + k + 1],
                                                             in1=c.rstd[:, 0:W], op0=ALU.mult, op1=ALU.mult),
                 [srcB[k], c.rstdB, B_const], [dstB[k]])

    def residual(c, W, gcol, half):
        for k in range(KT):
            t = c.tp % 2
            c.tp += 1
            S.op("pool", lambda e, k=k, t=t: e.tensor_tensor(out=c.tmp[:, t, 0:W], in0=c.ft[:, k, 0:W], in1=c.rstd[:, 0:W], op=ALU.mult),
                 [c.ftB[k], c.rstdB], [c.tmpB[t]])
            if half:
                S.op("dve", lambda e, k=k, t=t: e.tensor_scalar(out=c.tmp[:, t, 0:W], in0=c.tmp[:, t, 0:W], scalar1=gsb[:, gcol + k:gcol + k + 1],
                                                                scalar2=0.5, op0=ALU.mult, op1=ALU.mult), [c.tmpB[t], B_const], [c.tmpB[t]])
            else:
                S.op("dve", lambda e, k=k, t=t: e.tensor_scalar(out=c.tmp[:, t, 0:W], in0=c.tmp[:, t, 0:W], scalar1=gsb[:, gcol + k:gcol + k + 1],
                                                                scalar2=None, op0=ALU.mult), [c.tmpB[t], B_const], [c.tmpB[t]])
            S.op("dve", lambda e, k=k, t=t: e.tensor_tensor(out=c.xt[:, k, 0:W], in0=c.xt[:, k, 0:W], in1=c.tmp[:, t, 0:W], op=ALU.add),
                 [c.tmpB[t], c.xtB[k]], [c.xtB[k]])

    def down_like(c, wname, nkt, rhs, rhsB, W, pbs, ss_pb):
        nhalf = 2 if nkt > 32 else 1
        kh = nkt // nhalf
        for o in range(KT):
            slots = [wload(c, wname, hh * kh * 128, kh, o * 128, 128) for hh in range(nhalf)]
            pb = pbs[o % len(pbs)]
            for k in range(nkt):
                s = slots[k // kh]
                kk = k % kh
                S.op("pe", lambda e, s=s, kk=kk, k=k, pb=pb: e.matmul(psum[pb][:, 0:W], c.ring[:, s, kk * 128:(kk + 1) * 128], rhs[:, k, 0:W],
                                                                     start=(k == 0), stop=(k == nkt - 1)),
                     [c.ringB[s], rhsB[k]], [PB[pb]], inc=(k == nkt - 1))
            S.op("dve", lambda e, o=o, pb=pb: e.tensor_copy(out=c.ft[:, o, 0:W], in_=psum[pb][:, 0:W]), [PB[pb]], [c.ftB[o]])
            i = c.sqp % 4
            c.sqp += 1
            S.op("act", lambda e, i=i, pb=pb: e.activation(out=c.sq[:, i, 0:W], in_=psum[pb][:, 0:W], func=AF.Square), [PB[pb]], [c.sqB[i]])
            import os
            DL = int(os.environ.get("DL", "0"))
            S.op("pe", lambda e, i=i, o=o: e.matmul(psum[ss_pb][:, 0:W], ones_b[:], c.sq[:, i, 0:W], start=(o == 0 or DL == 1), stop=(o == KT - 1 or DL == 1)),
                 [c.sqB[i], B_const], [PB[ss_pb]], inc=True)
        S.op("act", lambda e: e.activation(out=c.rstd[:, 0:W], in_=psum[ss_pb][:, 0:W], func=AF.Sqrt, scale=1.0 / D, bias=epsb[:]),
             [PB[ss_pb], B_const], [c.rstdB])
        S.op("dve", lambda e: e.reciprocal(out=c.rstd[:, 0:W], in_=c.rstd[:, 0:W]), [c.rstdB], [c.rstdB])

    def ffn(c, W, pre, gpre, gpost):
        norm_stats(c, c.xt, c.xtB, KT, W, 7, D)
        normalize(c, c.xt, c.xtB, c.xn, c.xnB, KT, W, gpre)
        import os
        F1S = int(os.environ.get("F1S", "9"))
        if F1S < 2:
            return
        for f in range(FT):
            def fns(s, f=f):
                if (pre + "g") in w_t:
                    return [lambda e: e.dma_start(out=c.ring[:, s, 0:2048], in_=w_t[pre + "g"][f, :, 0:2048]),
                            lambda e: e.dma_start(out=c.ring[:, s, 2048:4096], in_=w_t[pre + "u"][f, :, 0:2048])]
                return [lambda e: e.dma_start(out=c.ring[:, s, 0:2048].rearrange("p (k c) -> p k c", k=KT),
                                              in_=w_b[pre + "g"][:, f * 128:(f + 1) * 128].rearrange("(k p) c -> p k c", p=128)),
                        lambda e: e.dma_start(out=c.ring[:, s, 2048:4096].rearrange("p (k c) -> p k c", k=KT),
                                              in_=w_b[pre + "u"][:, f * 128:(f + 1) * 128].rearrange("(k p) c -> p k c", p=128))]
            s = ring_load(c, fns)
            pg = 2 * (f % 2)
            pu = pg + 1
            for k in range(KT):
                S.op("pe", lambda e, s=s, k=k, pg=pg: e.matmul(psum[pg][:, 0:W], c.ring[:, s, k * 128:(k + 1) * 128], c.xn[:, k, 0:W],
                                                               start=(k == 0), stop=(k == KT - 1)), [c.ringB[s], c.xnB[k]], [PB[pg]], inc=(k == KT - 1))
            for k in range(KT):
                S.op("pe", lambda e, s=s, k=k, pu=pu: e.matmul(psum[pu][:, 0:W], c.ring[:, s, 2048 + k * 128:2048 + (k + 1) * 128], c.xn[:, k, 0:W],
                                                               start=(k == 0), stop=(k == KT - 1)), [c.ringB[s], c.xnB[k]], [PB[pu]], inc=(k == KT - 1))
            t = c.tp % 2
            c.tp += 1
            S.op("act", lambda e, t=t, pg=pg: e.activation(out=c.tmp[:, t, 0:W], in_=psum[pg][:, 0:W], func=AF.Silu), [PB[pg]], [c.tmpB[t]])
            S.op("dve", lambda e, t=t, pu=pu, f=f: e.tensor_tensor(out=c.act[:, f, 0:W], in0=c.tmp[:, t, 0:W], in1=psum[pu][:, 0:W], op=ALU.mult),
                 [c.tmpB[t], PB[pu]], [c.actB[f]])
        if F1S < 3:
            return
        down_like(c, pre + "d", FT, c.act, c.actB, W, [4, 5], 6)
        if F1S < 4:
            return
        residual(c, W, gpost, True)

    def stage_store(c, rows, W, src_fn, srcBs, dst_ap_fn):
        i = c.sp_ % 4
        c.sp_ += 1
        src_fn(i)
        S.dma("pool", [lambda e: e.dma_start(out=dst_ap_fn(), in_=c.stg[0:rows, i, 0:W])], [c.stgB[i]], [], c.stgD[i])

    def proj_fm(c, W, col0, npairs, consume):
        for hp in range(npairs):
            s = wload(c, "win", 0, KT, col0 + hp * 256, 256)
            for hh in range(2):
                j = hp * 2 + hh
                pb = j % 4
                for k in range(KT):
                    S.op("pe", lambda e, s=s, k=k, hh=hh, pb=pb: e.matmul(psum[pb][:, 0:W], c.ring[:, s, k * 256 + hh * 128:k * 256 + hh * 128 + 128],
                                                                        c.xn[:, k, 0:W], start=(k == 0), stop=(k == KT - 1)),
                         [c.ringB[s], c.xnB[k]], [PB[pb]], inc=(k == KT - 1))
                consume(j, pb)

    def copy_store(c, rows, W, pb, dst_fn):
        def ev(i):
            S.op("act", lambda e: e.activation(out=c.stg[0:rows, i, 0:W], in_=psum[pb][0:rows, 0:W], func=AF.Copy), [PB[pb]], [c.stgB[i]])
        stage_store(c, rows, W, ev, None, dst_fn)

    def rope_store(c, W, src, srcB, pbr, cs, csB, dst_fn):
        S.op("pe", lambda e: e.matmul(psum[pbr][0:64, 0:W], rot_f[:], src, start=True, stop=True), [srcB, B_const], [PB[pbr]])
        t0_ = c.tp % 2
        c.tp += 1
        S.op("dve", lambda e: e.tensor_tensor(out=c.tmp[0:64, t0_, 0:W], in0=src, in1=cs[:, 0, 0:W], op=ALU.mult), [srcB, csB], [c.tmpB[t0_]])
        t1_ = c.tp % 2
        c.tp += 1
        S.op("dve", lambda e: e.tensor_tensor(out=c.tmp[0:64, t1_, 0:W], in0=psum[pbr][0:64, 0:W], in1=cs[:, 1, 0:W], op=ALU.mult),
             [PB[pbr], csB], [c.tmpB[t1_]])

        def ev(i):
            S.op("dve", lambda e: e.tensor_tensor(out=c.stg[0:64, i, 0:W], in0=c.tmp[0:64, t0_, 0:W], in1=c.tmp[0:64, t1_, 0:W], op=ALU.add),
                 [c.tmpB[t0_], c.tmpB[t1_]], [c.stgB[i]])
        stage_store(c, 64, W, ev, None, dst_fn)

    def tileA1(c, t, cs, csB, csD):
        W = 16 if t < 0 else 512
        kc0 = 0 if t < 0 else 16 + t * 512
        nblk = 1 if t < 0 else 4
        rows = 16 if t < 0 else 128
        src = metaT if t < 0 else xT[:, t * 512:(t + 1) * 512]
        S.dma("sp", [lambda e: e.dma_start(out=c.xt[:, :, 0:W], in_=src.rearrange("(k p) w -> p k w", p=128))], [], c.xtB, c.xtD)
        S.dma("sp", [lambda e: e.dma_start(out=cs[:, 0, 0:W], in_=cosk[:, kc0:kc0 + W]),
                     lambda e: e.dma_start(out=cs[:, 1, 0:W], in_=sink[:, kc0:kc0 + W])], [], [csB], csD)
        import os
        A1S = int(os.environ.get("A1S", "9"))
        if A1S >= 1:
            ffn(c, W, "f1", G_F1PRE, G_F1POST)
        if A1S < 2:
            return
        if t >= 0:
            S.dma("pool", [lambda e: e.dma_start(out=h1o[:, :, t * 128:(t + 1) * 128].rearrange("k p w -> p k w"), in_=c.xt[:, :, 0:128])],
                  c.xtB, [], c.xsD)
        norm_stats(c, c.xt, c.xtB, KT, W, 7, D)
        normalize(c, c.xt, c.xtB, c.xn, c.xnB, KT, W, G_MIXPRE)
        proj_fm(c, W, C_DK, 4, lambda h, pb: copy_store(c, 128, W, pb, lambda: kT_da[h, :, kc0:kc0 + W]))
        if A1S < 3:
            return
        for qd in range(4):
            s = wload(c, "win", 0, KT, C_DV + qd * 256, 256)
            for j in range(nblk):
                pb = j % 4
                for k in range(KT):
                    S.op("pe", lambda e, s=s, k=k, j=j, pb=pb: e.matmul(psum[pb][0:rows, 0:256], c.xn[:, k, j * 128:j * 128 + rows],
                                                                      c.ring[:, s, k * 256:(k + 1) * 256], start=(k == 0), stop=(k == KT - 1)),
                         [c.ringB[s], c.xnB[k]], [PB[pb]], inc=(k == KT - 1))
                blk = 0 if t < 0 else 1 + t * 4 + j
                copy_store(c, rows, 256, pb, lambda blk=blk, qd=qd: v_da[blk, 0:rows, qd * 256:(qd + 1) * 256])
        if A1S < 4:
            return
        proj_fm(c, W, C_CKV, 2, lambda j, pb: S.op("dve", lambda e: e.tensor_copy(out=c.ft[:, j, 0:W], in_=psum[pb][:, 0:W]), [PB[pb]], [c.ftB[j]]))
        s = wload(c, "win", 0, KT, C_KR, 64)
        for k in range(KT):
            S.op("pe", lambda e, k=k, s=s: e.matmul(psum[4][0:64, 0:W], c.ring[:, s, k * 64:(k + 1) * 64], c.xn[:, k, 0:W],
                                               start=(k == 0), stop=(k == KT - 1)), [c.ringB[s], c.xnB[k]], [PB[4]], inc=(k == KT - 1))
        S.op("dve", lambda e: e.tensor_copy(out=c.ft[0:64, 4, 0:W], in_=psum[4][0:64, 0:W]), [PB[4]], [c.ftB[4]])
        rope_store(c, W, c.ft[0:64, 4, 0:W], c.ftB[4], 5, cs, csB, lambda: kT_ro[:, kc0:kc0 + W])
        if A1S < 5:
            return
        norm_stats(c, c.ft, c.ftB, 4, W, 7, 512)
        normalize(c, c.ft, c.ftB, c.act, c.actB, 4, W, G_KVN)
        sl = [wload(c, "wukv", 0, 4, hf * 1024, 1024) for hf in range(2)]
        for h in range(NH):
            pb = h % 4
            s = sl[h // 4]
            hh = h % 4
            for k in range(4):
                S.op("pe", lambda e, k=k, s=s, hh=hh, pb=pb: e.matmul(psum[pb][:, 0:W], c.ring[:, s, k * 1024 + hh * 256:k * 1024 + hh * 256 + 128],
                                                                    c.act[:, k, 0:W], start=(k == 0), stop=(k == 3)),
                     [c.ringB[s], c.actB[k]], [PB[pb]], inc=(k == 3))
            copy_store(c, 128, W, pb, lambda h=h: kT_no[h, :, kc0:kc0 + W])
        for j in range(nblk):
            blk = 0 if t < 0 else 1 + t * 4 + j
            for half in range(2):
                pb = (2 * j + half) % 4
                s = sl[half]
                for k in range(4):
                    S.op("pe", lambda e, k=k, j=j, s=s, pb=pb: e.matmul(
                        psum[pb][0:rows, 0:512].rearrange("p (h c) -> p h c", h=4), c.act[:, k, j * 128:j * 128 + rows],
                        c.ring[:, s, k * 1024:(k + 1) * 1024].rearrange("p (h c) -> p h c", h=4)[:, :, 128:256], start=(k == 0), stop=(k == 3)),
                        [c.ringB[s], c.actB[k]], [PB[pb]], inc=(k == 3))
                copy_store(c, rows, 512, pb, lambda blk=blk, half=half: v_ml[blk, 0:rows, half * 512:(half + 1) * 512])

    with ExitStack() as ph:
        c = alloc_main(ph)
        cs = ph.enter_context(sbt("cs", [64, 2, 512], F32))
        if STOP < 1:
            NT = -1
        csB = Buf("cs")
        csD = S.dsem("cs")
        import os
        tl = [int(v) for v in os.environ["A1T"].split(",")] if "A1T" in os.environ else list(range(-1, NT))
        for t in tl:
            tileA1(c, t, cs, csB, csD)
        S.barrier()

    def tileA2(c, u, wuq, B_wuq, cq, cqB, cqn, cqnB, qr, qrB, cs, csB, csD):
        W = 512
        oc = u * 512
        S.dma("pool", [lambda e: e.dma_start(out=c.xt[:], in_=h1o[:, :, oc:oc + 512].rearrange("k p w -> p k w"))], [], c.xtB, c.xtD)
        S.dma("pool", [lambda e: e.dma_start(out=cs[:, 0, :], in_=cosq[:, oc:oc + 512]),
                       lambda e: e.dma_start(out=cs[:, 1, :], in_=sinq[:, oc:oc + 512])], [], [csB], csD)
        norm_stats(c, c.xt, c.xtB, KT, W, 7, D)
        normalize(c, c.xt, c.xtB, c.xn, c.xnB, KT, W, G_MIXPRE)
        proj_fm(c, W, C_DQ, 4, lambda h, pb: copy_store(c, 128, W, pb, lambda: qT_da[h, :, oc:oc + 512]))
        proj_fm(c, W, C_CQ, 3, lambda j, pb: S.op("dve", lambda e: e.tensor_copy(out=cq[:, j, :], in_=psum[pb][:, 0:W]), [PB[pb]], [cqB[j]]))

        def gate_consume(j, pb):
            def ev(i):
                S.op("act", lambda e: e.activation(out=c.stg[:, i, 0:W], in_=psum[pb][:, 0:W], func=AF.Sigmoid, bias=gsb[:, G_BG + j:G_BG + j + 1]),
                     [PB[pb], B_const], [c.stgB[i]])
            stage_store(c, 128, W, ev, None, lambda: gat[j, :, oc:oc + 512])
        proj_fm(c, W, C_G, 16, gate_consume)
        norm_stats(c, cq, cqB, 6, W, 7, 768)
        normalize(c, cq, cqB, cqn, cqnB, 6, W, G_QN)
        for h in range(NH):
            pb = h % 4
            for k in range(6):
                S.op("pe", lambda e, k=k, h=h, pb=pb: e.matmul(psum[pb][:, 0:W], wuq[:, k, h * 192:h * 192 + 128], cqn[:, k, :], start=(k == 0), stop=(k == 5)),
                     [B_wuq, cqnB[k]], [PB[pb]], inc=(k == 5))
            copy_store(c, 128, W, pb, lambda h=h: qT_no[h, :, oc:oc + 512])
            pr = 4 + (h % 2) * 2
            for k in range(6):
                S.op("pe", lambda e, k=k, h=h, pr=pr: e.matmul(psum[pr][0:64, 0:W], wuq[:, k, h * 192 + 128:h * 192 + 192], cqn[:, k, :], start=(k == 0), stop=(k == 5)),
                     [B_wuq, cqnB[k]], [PB[pr]], inc=(k == 5))
            S.op("dve", lambda e, pr=pr: e.tensor_copy(out=qr[:, :], in_=psum[pr][0:64, 0:W]), [PB[pr]], [qrB])
            rope_store(c, W, qr[:, :], qrB, pr + 1, cs, csB, lambda h=h: qT_ro[h, :, oc:oc + 512])

    with ExitStack() as ph:
        c = alloc_main(ph, need_ffn=False)
        wuq = ph.enter_context(sbt("wuq", [128, 6, 1536], BF16))
        B_wuq = Buf("wuq")
        cq = ph.enter_context(sbt("cq", [128, 6, 512], F32))
        cqB = [Buf("cq%d" % k) for k in range(6)]
        cqn = ph.enter_context(sbt("cqn", [128, 6, 512], BF16))
        cqnB = [Buf("cqn%d" % k) for k in range(6)]
        qr = ph.enter_context(sbt("qr", [64, 512], F32))
        qrB = Buf("qr")
        cs = ph.enter_context(sbt("cs2", [64, 2, 512], F32))
        csB = Buf("cs")
        csD = S.dsem("cs2")
        S.dma("sp", [lambda e: e.dma_start(out=wuq[:], in_=w_b["wuq"].rearrange("(k p) c -> p k c", p=128))], [], [B_wuq], S.dsem("wuq"))
        for u in range(NOT_ if STOP >= 2 else 0):
            tileA2(c, u, wuq, B_wuq, cq, cqB, cqn, cqnB, qr, qrB, cs, csB, csD)
        S.barrier()

    with ExitStack() as ph:
        bsel = ph.enter_context(sbt("bsel", [128, 9, NSLOT, 128], BF16))
        B_bsel = Buf("bsel")
        bm = ph.enter_context(sbt("bm", [16, 9, 128], BF16))
        B_bm = Buf("bm")
        with ExitStack() as p2:
            taug = p2.enter_context(sbt("taug", [33, 16], F32))
            ohs = p2.enter_context(sbt("ohs", [33, NSLOT * 256], F32))
            lh = p2.enter_context(sbt("lh", [33, 128], F32))
            tv = p2.enter_context(sbt("tv", [128, NSLOT * 256], F32))
            bf = p2.enter_context(sbt("bf", [128, NSLOT, 128], F32))
            bmf = p2.enter_context(sbt("bmf", [16, 128], F32))
            B_t = Buf("taug")
            B_lh = Buf("lh")
            B_tv = Buf("tv")
            B_bf = Buf("bf")
            B_toep = Buf("toep")
            B_bmf = Buf("bmf")
            d_t = S.dsem("toep")
            d_t2 = S.dsem("toep2")
            d_t3 = S.dsem("toep3")
            S.op("dve", lambda e: e.memset(taug[:], 0.0), [], [B_t])
            S.op("dve", lambda e: e.memset(taug[32:33, :], NEG), [B_t], [B_t])
            S.dma("sp", [lambda e: e.dma_start(out=taug[0:32, 0:8], in_=tab), lambda e: e.dma_start(out=ohs[:], in_=oh)], [B_t], [B_t], d_t)

            def mk_bias(h):
                S.op("dve", lambda e: e.memset(lh[:], 1.0), [], [B_lh])
                S.op("dve", lambda e: e.tensor_scalar(out=lh[:], in0=lh[:], scalar1=taug[:, h:h + 1], scalar2=None, op0=ALU.mult), [B_t, B_lh], [B_lh])
                ncol = NSLOT * 256
                for c0 in range(0, ncol, 512):
                    c1 = min(ncol, c0 + 512)
                    pb = (c0 // 512) % 4
                    S.op("pe", lambda e, c0=c0, c1=c1, pb=pb: e.matmul(psum[pb][:, 0:c1 - c0], lh[:], ohs[:, c0:c1], start=True, stop=True), [B_lh, B_t], [PB[pb]])
                    S.op("dve", lambda e, c0=c0, c1=c1, pb=pb: e.tensor_scalar(out=tv[:, c0:c1], in0=psum[pb][:, 0:c1 - c0], scalar1=8.0, scalar2=None, op0=ALU.mult),
                         [PB[pb]], [B_tv])
                S.dma("sp", [lambda e: e.dma_start(out=toep[h * NSLOT:(h + 1) * NSLOT, :, :].rearrange("s p c -> p s c"),
                                                   in_=tv[:].rearrange("p (s c) -> p s c", s=NSLOT))], [B_tv], [B_toep], d_t)
                S.dma("sp", [lambda e, sl=sl: e.dma_start(
                    out=bf[:, sl, :], in_=bass.AP(tensor=toep.tensor, offset=(h * NSLOT + sl) * 128 * 256 + 127, ap=[[255, 128], [1, 128]]))
                    for sl in range(NSLOT)], [B_toep], [B_bf], d_t2)
                S.op("dve", lambda e: e.tensor_copy(out=bsel[:, h, :, :], in_=bf[:]), [B_bf], [B_bsel])
                S.dma("sp", [lambda e: e.dma_start(
                    out=bmf[:, :], in_=bass.AP(tensor=toep.tensor, offset=(h * NSLOT + 8) * 128 * 256 + 127 + 112 * 255, ap=[[255, 16], [1, 128]]))],
                    [B_toep], [B_bmf], d_t3)
                S.op("dve", lambda e: e.tensor_copy(out=bm[:, h, :], in_=bmf[:, :]), [B_bmf], [B_bm])
            for h in range(9 if STOP >= 3 else 0):
                mk_bias(h)
            S.barrier()

        kT = ph.enter_context(sbt("kTs", [128, 1, NK], BF16))
        vv = ph.enter_context(sbt("vvs", [128, 1, NB, 128], BF16))
        qq = ph.enter_context(sbt("qqs", [128, 1, NOWN], BF16))
        kTh = ph.enter_context(sbt("kTh", [64, 1, 2, NK], BF16))
        qqh = ph.enter_context(sbt("qqh", [64, 1, 2, NOWN], BF16))
        kro = ph.enter_context(sbt("kro", [64, NK], BF16))
        qro = ph.enter_context(sbt("qro", [64, 1, NOWN], BF16))
        B_kv = [Buf("kv0"), Buf("kv1")]
        D_kv = [S.dsem("kv"), S.dsem("kv")]
        B_kro = Buf("kro")
        pt = ph.enter_context(sbt("pt", [128, 4, 512], BF16))
        ptB = [Buf("pt%d" % i) for i in range(4)]
        fin = ph.enter_context(sbt("fin", [128, 4, 512], F32))
        finB = [Buf("fin%d" % i) for i in range(4)]
        sqf = ph.enter_context(sbt("sqf", [128, 512], BF16))
        sqfB = Buf("sqf")
        acc = ph.enter_context(sbt("acc", [128, 2, 512], F32))
        accB = [Buf("acc0"), Buf("acc1")]
        ones_f = ph.enter_context(sbt("ones_f", [128, 128], F32))
        B_onesf = Buf("ones_f")
        S.op("dve", lambda e: e.memset(ones_f[:], 1.0), [], [B_onesf])
        ystg = ph.enter_context(sbt("ystg", [128, 2, 512], BF16))
        ystgB = [Buf("y0"), Buf("y1")]
        ystgD = [S.dsem("y"), S.dsem("y")]
        S.dma("sp", [lambda e: e.dma_start(out=kro[:], in_=kT_ro)], [], [B_kro], S.dsem("kro"))
        NQT = NOWN // 512
        ycnt = [0]

        def attn_tile(mode, h, bi, tq):
            nS = 2 if mode == "da" else 1
            bh = h if mode == "da" else 8
            scale = 0.125 if mode == "da" else 192.0 ** -0.5
            kbs = [(0, 0, 16, -1)]
            for g in range(0, 4 * tq + 4):
                for i in range(4):
                    kbs.append((1 + 4 * g + i, 16 + (4 * g + i) * 128, 128, g))
            if mode == "da":
                sbank = [[0, 1], [2, 3]]
                obank = [4, 5]
                lbank = [6, 7]
            else:
                sbank = [[0], [1]]
                obank = [4]
                lbank = [6]
            kbs = kbs[:int(os.environ.get("ATT_STEPS", 10000))]
            nst = len(kbs)
            qe = tq * 512 + 512

            def qk(step):
                blk, kc, nk, g = kbs[step]
                par = step % 2
                m0 = max(g, 4 * tq)
                c0 = (m0 - 4 * tq) * 128
                qc0 = tq * 512 + c0
                extra = []
                if g < 0:
                    if tq == 0:
                        extra.append(("m", 0, 0))
                else:
                    for m in range(m0, 4 * tq + 4):
                        if g in (m - 1, m):
                            extra.append(("b", (m - 4 * tq) * 128, 4 * (g - m + 1) + (blk - 1 - 4 * g)))
                nex = len(extra)
                for si in range(nS):
                    pb = sbank[par][si]
                    if mode == "da":
                        p0 = 64 * si
                        S.op("pe", lambda e, si=si, pb=pb: e.matmul(psum[pb][0:nk, c0:512], kTh[:, bi, si, kc:kc + nk], qqh[:, bi, si, qc0:qe],
                                                                  start=True, stop=(nex == 0)), [B_kv[bi]], [PB[pb]], inc=(nex == 0))
                    else:
                        S.op("pe", lambda e, pb=pb: e.matmul(psum[pb][0:nk, c0:512], kT[:, bi, kc:kc + nk], qq[:, bi, qc0:qe], start=True, stop=False),
                             [B_kv[bi]], [PB[pb]], inc=False)
                        S.op("pe", lambda e, pb=pb: e.matmul(psum[pb][0:nk, c0:512], kro[:, kc:kc + nk], qro[:, bi, qc0:qe], start=False, stop=(nex == 0)),
                             [B_kv[bi], B_kro], [PB[pb]], inc=(nex == 0))
                    for xi, ex in enumerate(extra):
                        last = xi == nex - 1
                        if ex[0] == "m":
                            S.op("pe", lambda e, pb=pb, last=last: e.matmul(psum[pb][0:16, 0:128], ident_b[0:16, 0:16], bm[:, bh, :], start=False, stop=last),
                                 [B_bm, B_const], [PB[pb]], inc=last)
                        else:
                            _, cc, sl = ex
                            S.op("pe", lambda e, pb=pb, last=last, cc=cc, sl=sl: e.matmul(psum[pb][:, cc:cc + 128], ident_b[:], bsel[:, bh, sl, :], start=False, stop=last),
                                 [B_bsel, B_const], [PB[pb]], inc=last)
                    pi = par * 2 + si
                    S.op("act", lambda e, pb=pb, pi=pi: e.activation(out=pt[0:nk, pi, c0:512], in_=psum[pb][0:nk, c0:512], func=AF.Exp, scale=scale),
                         [PB[pb]], [ptB[pi]])
                return (blk, nk, c0, par)

            def pv(info, first, lastk):
                blk, nk, c0, par = info
                for si in range(nS):
                    pi = par * 2 + si
                    S.op("pe", lambda e, si=si, pi=pi: e.matmul(psum[obank[si]][:, c0:512], vv[0:nk, bi, blk, :], pt[0:nk, pi, c0:512], start=first, stop=lastk),
                         [B_kv[bi], ptB[pi]], [PB[obank[si]]], inc=True)
                    S.op("dve" if si == 0 else "pool",
                         lambda e, si=si, pi=pi: e.tensor_tensor(out=acc[0:nk, si, c0:512], in0=acc[0:nk, si, c0:512], in1=pt[0:nk, pi, c0:512], op=ALU.add),
                         [ptB[pi], accB[si]], [accB[si]])

            for si in range(nS):
                S.op("dve" if si == 0 else "pool", lambda e, si=si: e.memset(acc[:, si, :], 0.0), [], [accB[si]])
            pend = None
            for step in range(nst + 1):
                cur = qk(step) if step < nst else None
                if pend is not None:
                    pv(pend, step == 1, step == nst)
                pend = cur
            oc = tq * 512
            for si in range(nS):
                S.op("pe", lambda e, si=si: e.matmul(psum[lbank[si]][:, :], ones_f[:], acc[:, si, :], start=True, stop=True),
                     [accB[si], B_onesf], [PB[lbank[si]]])
                S.op("dve", lambda e, si=si: e.reciprocal(out=fin[:, si, :], in_=psum[lbank[si]][:, :]), [PB[lbank[si]]], [finB[si]])
                S.op("dve", lambda e, si=si: e.tensor_tensor(out=fin[:, si, :], in0=psum[obank[si]][:, :], in1=fin[:, si, :], op=ALU.mult),
                     [PB[obank[si]], finB[si]], [finB[si]])
            yi = ycnt[0] % 2
            ycnt[0] += 1
            if mode == "da":
                S.op("dve", lambda e: e.scalar_tensor_tensor(out=fin[:, 2, :], in0=fin[:, 1, :], scalar=neglam[:], in1=fin[:, 0, :], op0=ALU.mult, op1=ALU.add),
                     [finB[0], finB[1], B_const], [finB[2]])
                S.op("act", lambda e: e.activation(out=sqf[:], in_=fin[:, 2, :], func=AF.Square), [finB[2]], [sqfB])
                S.op("pe", lambda e: e.matmul(psum[0][:, :], ones_b[:], sqf[:], start=True, stop=True), [sqfB, B_const], [PB[0]])
                S.op("act", lambda e: e.activation(out=fin[:, 3, :], in_=psum[0][:, :], func=AF.Sqrt, scale=1.0 / 128, bias=epsb[:]), [PB[0], B_const], [finB[3]])
                S.op("dve", lambda e: e.reciprocal(out=fin[:, 3, :], in_=fin[:, 3, :]), [finB[3]], [finB[3]])
                S.op("dve", lambda e: e.scalar_tensor_tensor(out=ystg[:, yi, :], in0=fin[:, 2, :], scalar=subg8[:], in1=fin[:, 3, :], op0=ALU.mult, op1=ALU.mult),
                     [finB[2], finB[3], B_const], [ystgB[yi]])
                S.dma("pool", [lambda e: e.dma_start(out=y_da[h, :, oc:oc + 512], in_=ystg[:, yi, :])], [ystgB[yi]], [], ystgD[yi])
            else:
                S.op("act", lambda e: e.activation(out=ystg[:, yi, :], in_=fin[:, 0, :], func=AF.Copy), [finB[0]], [ystgB[yi]])
                S.dma("pool", [lambda e: e.dma_start(out=y_ml[h, :, oc:oc + 512], in_=ystg[:, yi, :])], [ystgB[yi]], [], ystgD[yi])

        def attn_head(mode, h, bi):
            if mode == "da":
                fns = [lambda e: e.dma_start(out=kTh[:, bi, 0, :], in_=kT_da[h, 0:64, :]),
                       lambda e: e.dma_start(out=kTh[:, bi, 1, :], in_=kT_da[h, 64:128, :]),
                       lambda e: e.dma_start(out=vv[:, bi, :, :], in_=v_da[:, :, h * 128:(h + 1) * 128].rearrange("b p c -> p b c")),
                       lambda e: e.dma_start(out=qqh[:, bi, 0, :], in_=qT_da[h, 0:64, :]),
                       lambda e: e.dma_start(out=qqh[:, bi, 1, :], in_=qT_da[h, 64:128, :])]
            else:
                fns = [lambda e: e.dma_start(out=kT[:, bi, :], in_=kT_no[h]),
                       lambda e: e.dma_start(out=vv[:, bi, :, :], in_=v_ml[:, :, h * 128:(h + 1) * 128].rearrange("b p c -> p b c")),
                       lambda e: e.dma_start(out=qq[:, bi, :], in_=qT_no[h]),
                       lambda e: e.dma_start(out=qro[:, bi, :], in_=qT_ro[h])]
            S.dma("sp", fns, [], [B_kv[bi]], D_kv[bi])
            for tq in range(int(os.environ.get("ATT_TQ", NQT))):
                attn_tile(mode, h, bi, tq)

        hc = 0
        import os
        for mode in os.environ.get("ATT_MODES", "da,mla").split(","):
            for h in range(int(os.environ.get("ATT_H", NH)) if STOP >= 4 else 0):
                attn_head(mode, h, 0)
                hc += 1
        S.barrier()

    def tileC(c, u, ya, ym, yB, yD, gt, gtB, gtD, d_out):
        W = 512
        oc = u * 512
        S.dma("pool", [lambda e: e.dma_start(out=c.xt[:], in_=h1o[:, :, oc:oc + 512].rearrange("k p w -> p k w"))], [], c.xtB, c.xtD)
        S.dma("pool", [lambda e: e.dma_start(out=ya[:], in_=y_da[:, :, oc:oc + 512].rearrange("k p w -> p k w")),
                       lambda e: e.dma_start(out=ym[:], in_=y_ml[:, :, oc:oc + 512].rearrange("k p w -> p k w"))], [], [yB], yD)
        for o in range(KT):
            gi = o % 2
            S.dma("pool", [lambda e, o=o, gi=gi: e.dma_start(out=gt[:, gi, 0, :], in_=gat[o, :, oc:oc + 512]),
                           lambda e, o=o, gi=gi: e.dma_start(out=gt[:, gi, 1, :], in_=gat[16 + o, :, oc:oc + 512])], [], [gtB[gi]], gtD[gi])
            s1 = wload(c, "wbd", 0, 8, o * 128, 128)
            s2 = wload(c, "wbm", 0, 8, o * 128, 128)
            pa = 2 * (o % 2)
            pm = pa + 1
            for k in range(8):
                S.op("pe", lambda e, s1=s1, k=k, pa=pa: e.matmul(psum[pa][:, :], c.ring[:, s1, k * 128:(k + 1) * 128], ya[:, k, :], start=(k == 0), stop=(k == 7)),
                     [c.ringB[s1], yB], [PB[pa]], inc=(k == 7))
            for k in range(8):
                S.op("pe", lambda e, s2=s2, k=k, pm=pm: e.matmul(psum[pm][:, :], c.ring[:, s2, k * 128:(k + 1) * 128], ym[:, k, :], start=(k == 0), stop=(k == 7)),
                     [c.ringB[s2], yB], [PB[pm]], inc=(k == 7))
            t = c.tp % 2
            c.tp += 1
            S.op("dve", lambda e, t=t, pa=pa, gi=gi: e.tensor_tensor(out=c.tmp[:, t, :], in0=psum[pa][:, :], in1=gt[:, gi, 0, :], op=ALU.mult),
                 [PB[pa], gtB[gi]], [c.tmpB[t]])
            t2 = c.tp % 2
            c.tp += 1
            S.op("dve", lambda e, t2=t2, pm=pm, gi=gi: e.tensor_tensor(out=c.tmp[:, t2, :], in0=psum[pm][:, :], in1=gt[:, gi, 1, :], op=ALU.mult),
                 [PB[pm], gtB[gi]], [c.tmpB[t2]])
            S.op("dve", lambda e, t=t, t2=t2, o=o: e.tensor_tensor(out=c.xn[:, o, :], in0=c.tmp[:, t, :], in1=c.tmp[:, t2, :], op=ALU.add),
                 [c.tmpB[t], c.tmpB[t2]], [c.xnB[o]])
        down_like(c, "wo", KT, c.xn, c.xnB, W, [4, 5], 6)
        residual(c, W, G_MIXPOST, False)
        ffn(c, W, "f2", G_F2PRE, G_F2POST)
        S.dma("pool", [lambda e: e.dma_start(out=outT[:, oc:oc + 512].rearrange("(k p) w -> p k w", p=128), in_=c.xt[:])], c.xtB, [], d_out)

    with ExitStack() as ph:
        c = alloc_main(ph, need_stg=False)
        ya = ph.enter_context(sbt("ya", [128, 8, 512], BF16))
        ym = ph.enter_context(sbt("ym", [128, 8, 512], BF16))
        yB = Buf("y")
        yD = S.dsem("yload")
        gt = ph.enter_context(sbt("gt", [128, 2, 2, 512], BF16))
        gtB = [Buf("gt0"), Buf("gt1")]
        gtD = [S.dsem("gt"), S.dsem("gt")]
        d_out = S.dsem("out")
        for u in range(NOT_ if STOP >= 5 else 0):
            tileC(c, u, ya, ym, yB, yD, gt, gtB, gtD, d_out)
        S.barrier()

    with nc.Block() as block:
        @block.tensor
        def _(e):
            for f in S.q["pe"].prog:
                f(e)

        @block.scalar
        def _(e):
            for f in S.q["act"].prog:
                f(e)

        @block.vector
        def _(e):
            for f in S.q["dve"].prog:
                f(e)

        @block.gpsimd
        def _(e):
            for f in S.q["pool"].prog:
                f(e)

        @block.sync
        def _(e):
            for f in S.q["sp"].prog:
                f(e)
    es.close()
    build.last_sched = S
    return nc


def _bucket(rel):
    n = np.maximum(rel, 0)
    n_f = np.maximum(n, 1).astype(np.float32)
    large = 16 + (np.log(n_f / np.float32(16)) / np.float32(math.log(128 / 16)) * np.float32(16)).astype(np.int32)
    large = np.minimum(large, 31)
    return np.where(n < 16, n, large)


def _bucket_jax(rel):
    import jax.numpy as jnp
    n = jnp.maximum(jnp.asarray(rel, jnp.int32), 0)
    n_f = jnp.maximum(n, 1).astype(jnp.float32)
    large = 16 + (jnp.log(n_f / 16) / math.log(128 / 16) * 16).astype(jnp.int32)
    large = jnp.minimum(large, 31)
    return np.asarray(jnp.where(n < 16, n, large))


def host_prep(inputs, SEQ, DFF):
    x = np.asarray(inputs["x"], np.float32)
    B = x.shape[0]
    NT = SEQ // 512
    NK = 16 + SEQ
    maps = []
    half = 32
    inv = (10000.0 ** (-np.arange(half, dtype=np.float32) * 2.0 / 64)).astype(np.float32)
    g = lambda n: np.asarray(inputs[n], np.float32)[0]
    gains = np.concatenate(
        [g(n).reshape(-1, 128).T for n in ("ffn1_pre_g", "ffn1_post_g", "mix_pre_g", "mix_post_g", "ffn2_pre_g", "ffn2_post_g")]
        + [g("b_gate").reshape(32, 128).T, g("mla_q_norm_g").reshape(6, 128).T, g("mla_kv_norm_g").reshape(4, 128).T,
           g("da_sub_g").reshape(1, 128).T], axis=1).astype(np.float32)
    lamv = np.concatenate([g("da_lambda_q1"), g("da_lambda_k1"), g("da_lambda_q2"), g("da_lambda_k2")])[None, :].repeat(128, 0).astype(np.float32)
    rot = np.zeros((64, 64), np.float32)
    for i in range(32):
        rot[32 + i, i] = -1.0
        rot[i, 32 + i] = 1.0
    ident = np.eye(128, dtype=np.float32)
    wmap = {"f1g": "ffn1_w_gate", "f1u": "ffn1_w_up", "f1d": "ffn1_w_down", "f2g": "ffn2_w_gate", "f2u": "ffn2_w_up", "f2d": "ffn2_w_down",
            "win": "w_in", "wuq": "mla_w_uq", "wukv": "mla_w_ukv", "wbd": "w_branch_da", "wbm": "w_branch_mla", "wo": "w_out"}
    import os
    shared = {"wsmall": (np.random.RandomState(0).randn(2048, 2048) / 45.0).astype(np.float32)} if "DBG_NOW" in os.environ else {"w_" + k: np.ascontiguousarray(np.asarray(inputs[v], np.float32)[0]) for k, v in wmap.items()}
    shared.update({"gains": gains, "lamv": lamv, "tab": np.asarray(inputs["rel_bias_table"], np.float32), "ident": ident, "rot": rot,
                   "metaT": np.ascontiguousarray(np.asarray(inputs["meta_tokens"], np.float32).T)})
    deltas = np.arange(256) - 127
    for core in range(8):
        b, c = core // 4, core % 4
        perm = [c] + [i for i in range(4) if i != c]
        order = np.concatenate([np.arange((4 * t + perm[i]) * 128, (4 * t + perm[i]) * 128 + 128) for t in range(NT) for i in range(4)])
        xT = np.ascontiguousarray(x[b][order].T)
        kpos = np.concatenate([np.arange(16), 16 + order]).astype(np.float32)
        ang = kpos[None, :] * inv[:, None]
        cosk = np.concatenate([np.cos(ang), np.cos(ang)], 0).astype(np.float32)
        sink = np.concatenate([np.sin(ang), np.sin(ang)], 0).astype(np.float32)
        own = np.concatenate([np.arange(16 + t * 512, 16 + t * 512 + 128) for t in range(NT)])
        cosq, sinq = np.ascontiguousarray(cosk[:, own]), np.ascontiguousarray(sink[:, own])
        oh = np.zeros((33, NSLOT * 256), np.float32)
        for sl in range(NSLOT):
            if sl < 8:
                gg, i = sl // 4, sl % 4
                dist = (4 + c) - (4 * gg + perm[i])
            else:
                dist = c + 1
            rel = 128 * dist + deltas
            bk = _bucket(rel)
            for j in range(255):
                if rel[j] < 0:
                    oh[32, sl * 256 + j] = 1.0
                else:
                    oh[bk[j], sl * 256 + j] += 1.0
                    oh[31, sl * 256 + j] -= 1.0
        m = dict(shared)
        m.update({"xT": xT, "cosk": cosk, "sink": sink, "cosq": cosq, "sinq": sinq, "oh": oh})
        maps.append(m)
    return maps


def host_post(results, B, SEQ):
    NT = SEQ // 512
    out = np.zeros((B, SEQ, D), np.float32)
    for core in range(8):
        b, c = core // 4, core % 4
        oT = np.asarray(results[core]["outT"])
        o = oT.T.reshape(NT, 128, D)
        for m in range(NT):
            out[b, (4 * m + c) * 128:(4 * m + c + 1) * 128] = o[m]
    return out


_CACHE = {}


def run(inputs, SEQ, DFF, STOP=99):
    key = (SEQ, DFF, STOP)
    if key not in _CACHE:
        _CACHE[key] = build(SEQ, DFF, STOP)
    nc = _CACHE[key]
    maps = host_prep(inputs, SEQ, DFF)
    res = run_bass_kernel_spmd(nc, maps, core_ids=list(range(8)))
    run.last_results = res.results
    return host_post(res.results, 2, SEQ)


def kernel(**inputs):
    return run(inputs, 8192, 5632)
```
